# Optimizing a Trainium2 kernel written in Bass

```python
import math
import jax
import jax.numpy as jnp
from jax import lax
import numpy as np

D_MODEL = 1024
BATCH = 16
SEQ = 2048
DEPTH = 2

GRID_W = 64
CTX_LEN = 256
Q_BLOCK = 128
ROPE_THETA = 10000.0
EPS = 1e-6
N_MOD = 6

DIFF_HEADS = 4
DIFF_HEAD_DIM = 64
DIFF_V_DIM = 2 * DIFF_HEAD_DIM
GQA_Q_HEADS = 8
GQA_KV_HEADS = 2
GQA_GROUP = GQA_Q_HEADS // GQA_KV_HEADS
GQA_HEAD_DIM = 64
MLA_HEADS = 8
MLA_Q_LORA = 384
MLA_KV_LORA = 256
MLA_NOPE_DIM = 64
MLA_ROPE_DIM = 32
MLA_QK_DIM = MLA_NOPE_DIM + MLA_ROPE_DIM
MLA_V_DIM = 64

N_BRANCHES = 3
DIFF_WIDTH = DIFF_HEADS * DIFF_V_DIM
GQA_WIDTH = GQA_Q_HEADS * GQA_HEAD_DIM
MLA_WIDTH = MLA_HEADS * MLA_V_DIM
FFN_HIDDEN = -((-8 * D_MODEL) // (3 * 256)) * 256

IN_SIZES = (
    DIFF_HEADS * 2 * DIFF_HEAD_DIM,
    DIFF_HEADS * 2 * DIFF_HEAD_DIM,
    DIFF_WIDTH,
    GQA_Q_HEADS * GQA_HEAD_DIM,
    GQA_KV_HEADS * GQA_HEAD_DIM,
    GQA_KV_HEADS * GQA_HEAD_DIM,
    MLA_Q_LORA,
    MLA_KV_LORA,
    MLA_ROPE_DIM,
    N_BRANCHES * D_MODEL,
)
IN_WIDTH = sum(IN_SIZES)
IN_SPLITS = [sum(IN_SIZES[:i + 1]) for i in range(len(IN_SIZES) - 1)]

kernel_name = 'hybrid_diff_gqa_mla_prefix_dit_block'


def _rmsnorm(x, g):
    xf = x.astype(jnp.float32)
    y = xf * lax.rsqrt(jnp.mean(xf * xf, axis=-1, keepdims=True) + EPS)
    return (y * g.astype(jnp.float32)).astype(x.dtype)


def _modulate(h, shift, scale):
    return h * (1.0 + scale) + shift


def _rope_angles(row, col, dim):
    a = dim // 2
    freqs = ROPE_THETA ** (-jnp.arange(0, a, 2, dtype=jnp.float32) / a)
    return (row.astype(jnp.float32)[:, None] * freqs, col.astype(jnp.float32)[:, None] * freqs)


def _rope1d(x, ang):
    ang = ang.reshape(ang.shape[:1] + (1,) * (x.ndim - 3) + ang.shape[1:])
    cos, sin = jnp.cos(ang), jnp.sin(ang)
    x1, x2 = jnp.split(x.astype(jnp.float32), 2, axis=-1)
    return jnp.concatenate([x1 * cos - x2 * sin, x1 * sin + x2 * cos], axis=-1).astype(x.dtype)


def _rope2d(x, angs):
    ang_row, ang_col = angs
    a = x.shape[-1] // 2
    return jnp.concatenate([_rope1d(x[..., :a], ang_row), _rope1d(x[..., a:], ang_col)], axis=-1)


def _sweep_queries(fn, q):
    B, L = q.shape[:2]
    nb = L // Q_BLOCK
    qb = jnp.moveaxis(q.reshape((B, nb, Q_BLOCK) + q.shape[2:]), 1, 0)
    out = jnp.moveaxis(lax.map(fn, qb), 0, 1)
    return out.reshape((B, L) + out.shape[3:])


def _diff_attn_block(q, k, v, lam):
    s = jnp.einsum('bqhmd,bkhmd->bhmqk', q, k).astype(jnp.float32) * (DIFF_HEAD_DIM ** -0.5)
    p = jax.nn.softmax(s, axis=-1)
    w = (p[:, :, 0] - lam * p[:, :, 1]).astype(v.dtype)
    return jnp.einsum('bhqk,bkhe->bqhe', w, v)


def _gqa_block(q, k, v, scale):
    s = jnp.einsum('bqhgd,bkhd->bhgqk', q, k).astype(jnp.float32) * scale
    p = jax.nn.softmax(s, axis=-1).astype(v.dtype)
    return jnp.einsum('bhgqk,bkhe->bqhge', p, v)


def _branch_inputs(h, angs, p):
    B, L = h.shape[:2]
    z = h @ p['w_in']
    dq, dk, dv, gq, gk, gv, mcq, mckv, mkr, gt = jnp.split(z, IN_SPLITS, axis=-1)
    dq = dq.reshape(B, L, DIFF_HEADS, 2, DIFF_HEAD_DIM)
    dk = dk.reshape(B, L, DIFF_HEADS, 2, DIFF_HEAD_DIM)
    dv = dv.reshape(B, L, DIFF_HEADS, DIFF_V_DIM)
    gq = _rmsnorm(gq.reshape(B, L, GQA_KV_HEADS, GQA_GROUP, GQA_HEAD_DIM), p['g_gqa_q'])
    gk = _rmsnorm(gk.reshape(B, L, GQA_KV_HEADS, GQA_HEAD_DIM), p['g_gqa_k'])
    gv = gv.reshape(B, L, GQA_KV_HEADS, GQA_HEAD_DIM)
    mq = (_rmsnorm(mcq, p['g_mla_q']) @ p['w_mla_uq']).reshape(B, L, MLA_HEADS, MLA_QK_DIM)
    mkv = (_rmsnorm(mckv, p['g_mla_kv']) @ p['w_mla_ukv']).reshape(B, L, MLA_HEADS, MLA_NOPE_DIM + MLA_V_DIM)
    mq_nope, mq_rope = jnp.split(mq, [MLA_NOPE_DIM], axis=-1)
    mk_nope, mv = jnp.split(mkv, [MLA_NOPE_DIM], axis=-1)
    if angs is not None:
        angs_qk, angs_mla = angs
        dq = _rope2d(dq, angs_qk)
        dk = _rope2d(dk, angs_qk)
        gq = _rope2d(gq, angs_qk)
        gk = _rope2d(gk, angs_qk)
        mq_rope = _rope2d(mq_rope, angs_mla)
        mkr = _rope2d(mkr, angs_mla)
    mq = jnp.concatenate([mq_nope, mq_rope], axis=-1)[:, :, :, None, :]
    mk = jnp.concatenate(
        [mk_nope, jnp.broadcast_to(mkr[:, :, None, :], (B, L, MLA_HEADS, MLA_ROPE_DIM))], axis=-1)
    return (dq, gq, mq), (dk, dv, gk, gv, mk, mv), gt


def _mix_branches(queries, keys, gates_pre, lam, lam_init, p):
    dq, gq, mq = queries
    dk, dv, gk, gv, mk, mv = keys
    B, L = dq.shape[:2]
    o_diff = _sweep_queries(lambda qb: _diff_attn_block(qb, dk, dv, lam), dq)
    o_diff = _rmsnorm(o_diff, p['g_diff_out']) * (1.0 - lam_init)
    o_gqa = _sweep_queries(lambda qb: _gqa_block(qb, gk, gv, GQA_HEAD_DIM ** -0.5), gq)
    o_mla = _sweep_queries(lambda qb: _gqa_block(qb, mk, mv, MLA_QK_DIM ** -0.5), mq)
    gate = jax.nn.sigmoid((gates_pre + p['b_gate']).astype(jnp.float32)).astype(gates_pre.dtype)
    gate = gate.reshape(B, L, N_BRANCHES, D_MODEL)
    y = (gate[:, :, 0] * (o_diff.reshape(B, L, DIFF_WIDTH) @ p['w_br_diff'])
         + gate[:, :, 1] * (o_gqa.reshape(B, L, GQA_WIDTH) @ p['w_br_gqa'])
         + gate[:, :, 2] * (o_mla.reshape(B, L, MLA_WIDTH) @ p['w_br_mla']))
    return y @ p['w_out']


def _swiglu(h, w_in, w_out):
    g, u = jnp.split(h @ w_in, 2, axis=-1)
    return (jax.nn.silu(g) * u) @ w_out


def _lambda_init(layer):
    return 0.8 - 0.6 * math.exp(-0.3 * layer)


def setup_inputs(seed: int = 0) -> dict:
    key = jax.random.key(seed)
    ks = jax.random.split(key, 28)
    f32 = jnp.float32

    def nrm(k, shape):
        return jax.random.normal(k, shape, f32)

    return {
        'x': nrm(ks[0], (BATCH, SEQ, D_MODEL)),
        'c': nrm(ks[1], (BATCH, D_MODEL)),
        'ctx': nrm(ks[2], (BATCH, CTX_LEN, D_MODEL)),
        'c_ctx': nrm(ks[3], (D_MODEL,)),
        'w_mod': nrm(ks[4], (DEPTH, D_MODEL, N_MOD * D_MODEL)) * (0.5 * D_MODEL ** -0.5),
        'b_mod': 0.02 * nrm(ks[5], (DEPTH, N_MOD * D_MODEL)),
        'g_norm1': 1.0 + 0.02 * nrm(ks[6], (DEPTH, D_MODEL)),
        'w_in': nrm(ks[7], (DEPTH, D_MODEL, IN_WIDTH)) * D_MODEL ** -0.5,
        'b_gate': 0.02 * nrm(ks[8], (DEPTH, N_BRANCHES * D_MODEL)),
        'lam_q1': 0.1 * nrm(ks[9], (DEPTH, DIFF_HEAD_DIM)),
        'lam_k1': 0.1 * nrm(ks[10], (DEPTH, DIFF_HEAD_DIM)),
        'lam_q2': 0.1 * nrm(ks[11], (DEPTH, DIFF_HEAD_DIM)),
        'lam_k2': 0.1 * nrm(ks[12], (DEPTH, DIFF_HEAD_DIM)),
        'g_diff_out': 1.0 + 0.02 * nrm(ks[13], (DEPTH, DIFF_V_DIM)),
        'g_gqa_q': 1.0 + 0.02 * nrm(ks[14], (DEPTH, GQA_HEAD_DIM)),
        'g_gqa_k': 1.0 + 0.02 * nrm(ks[15], (DEPTH, GQA_HEAD_DIM)),
        'g_mla_q': 1.0 + 0.02 * nrm(ks[16], (DEPTH, MLA_Q_LORA)),
        'w_mla_uq': nrm(ks[17], (DEPTH, MLA_Q_LORA, MLA_HEADS * MLA_QK_DIM)) * MLA_Q_LORA ** -0.5,
        'g_mla_kv': 1.0 + 0.02 * nrm(ks[18], (DEPTH, MLA_KV_LORA)),
        'w_mla_ukv': nrm(ks[19], (DEPTH, MLA_KV_LORA, MLA_HEADS * (MLA_NOPE_DIM + MLA_V_DIM))) * MLA_KV_LORA ** -0.5,
        'w_br_diff': nrm(ks[20], (DEPTH, DIFF_WIDTH, D_MODEL)) * DIFF_WIDTH ** -0.5,
        'w_br_gqa': nrm(ks[21], (DEPTH, GQA_WIDTH, D_MODEL)) * GQA_WIDTH ** -0.5,
        'w_br_mla': nrm(ks[22], (DEPTH, MLA_WIDTH, D_MODEL)) * MLA_WIDTH ** -0.5,
        'w_out': nrm(ks[23], (DEPTH, D_MODEL, D_MODEL)) * D_MODEL ** -0.5,
        'g_norm2': 1.0 + 0.02 * nrm(ks[24], (DEPTH, D_MODEL)),
        'w_ffn_in': nrm(ks[25], (DEPTH, D_MODEL, 2 * FFN_HIDDEN)) * D_MODEL ** -0.5,
        'w_ffn_out': nrm(ks[26], (DEPTH, FFN_HIDDEN, D_MODEL)) * FFN_HIDDEN ** -0.5,
        'g_final': 1.0 + 0.02 * nrm(ks[27], (D_MODEL,)),
    }


def reference(x, c, ctx, c_ctx, w_mod, b_mod, g_norm1, w_in, b_gate, lam_q1, lam_k1, lam_q2, lam_k2,
              g_diff_out, g_gqa_q, g_gqa_k, g_mla_q, w_mla_uq, g_mla_kv, w_mla_ukv,
              w_br_diff, w_br_gqa, w_br_mla, w_out, g_norm2, w_ffn_in, w_ffn_out, g_final):
    f32 = jnp.float32
    L = x.shape[1]
    rows = L // GRID_W
    row = jnp.repeat(jnp.arange(rows, dtype=jnp.int32), GRID_W)
    col = jnp.tile(jnp.arange(GRID_W, dtype=jnp.int32), rows)
    angs_qk = _rope_angles(row, col, DIFF_HEAD_DIM)
    angs_mla = _rope_angles(row, col, MLA_ROPE_DIM)
    xc = ctx
    for l in range(DEPTH):
        p = {
            'w_in': w_in[l], 'b_gate': b_gate[l], 'g_diff_out': g_diff_out[l],
            'g_gqa_q': g_gqa_q[l], 'g_gqa_k': g_gqa_k[l],
            'g_mla_q': g_mla_q[l], 'w_mla_uq': w_mla_uq[l], 'g_mla_kv': g_mla_kv[l], 'w_mla_ukv': w_mla_ukv[l],
            'w_br_diff': w_br_diff[l], 'w_br_gqa': w_br_gqa[l], 'w_br_mla': w_br_mla[l], 'w_out': w_out[l],
        }
        lam_init = _lambda_init(l)
        lam = (jnp.exp(jnp.sum(lam_q1[l].astype(f32) * lam_k1[l].astype(f32)))
               - jnp.exp(jnp.sum(lam_q2[l].astype(f32) * lam_k2[l].astype(f32))) + lam_init)
        mod = jnp.split((jax.nn.silu(c) @ w_mod[l] + b_mod[l])[:, None, :], N_MOD, axis=-1)
        mod_c = jnp.split((jax.nn.silu(c_ctx) @ w_mod[l] + b_mod[l])[None, None, :], N_MOD, axis=-1)
        h = _modulate(_rmsnorm(x, g_norm1[l]), mod[0], mod[1])
        hc = _modulate(_rmsnorm(xc, g_norm1[l]), mod_c[0], mod_c[1])
        q_lat, k_lat, gt_lat = _branch_inputs(h, (angs_qk, angs_mla), p)
        q_ctx, k_ctx, gt_ctx = _branch_inputs(hc, None, p)
        k_all = tuple(jnp.concatenate([kc, kl], axis=1) for kc, kl in zip(k_ctx, k_lat))
        x = x + mod[2] * _mix_branches(q_lat, k_all, gt_lat, lam, lam_init, p)
        x = x + mod[5] * _swiglu(_modulate(_rmsnorm(x, g_norm2[l]), mod[3], mod[4]), w_ffn_in[l], w_ffn_out[l])
        if l < DEPTH - 1:
            xc = xc + mod_c[2] * _mix_branches(q_ctx, k_ctx, gt_ctx, lam, lam_init, p)
            xc = xc + mod_c[5] * _swiglu(_modulate(_rmsnorm(xc, g_norm2[l]), mod_c[3], mod_c[4]),
                                         w_ffn_in[l], w_ffn_out[l])
    return _rmsnorm(x, g_final)
```

```python
import math
import numpy as np
import concourse.bass as bass
import concourse.mybir as mybir
from concourse.bass_utils import run_bass_kernel_spmd

F32 = mybir.dt.float32
BF16 = mybir.dt.bfloat16
AF = mybir.ActivationFunctionType
ALU = mybir.AluOpType
AX = mybir.AxisListType

D = 1024
SEQ = 2048
CTX = 256
NKEY = SEQ + CTX
EPS = 1e-6
FFN = 2816
THETA = 10000.0
WA_COLS = 1568
WQ_COLS = 1408


class Res:
    __slots__ = ("name", "w", "r")

    def __init__(self, name):
        self.name = name
        self.w = None
        self.r = {}


class Ins:
    __slots__ = ("eng", "fn", "deps", "sig", "tok", "dma", "dsem")

    def __init__(self, eng, fn, dma=False):
        self.eng = eng
        self.fn = fn
        self.deps = []
        self.sig = False
        self.tok = None
        self.dma = dma
        self.dsem = None


class Prog:
    def __init__(self, nc):
        self.nc = nc
        self.E = {"pe": nc.tensor, "act": nc.scalar, "dve": nc.vector, "pool": nc.gpsimd, "sp": nc.sync}
        self.ins = []
        self.res = {}
        self.dsems = {}
        self.outs = []

    def R(self, name):
        r = self.res.get(name)
        if r is None:
            r = self.res[name] = Res(name)
        return r

    def _track(self, I, reads, writes):
        deps = I.deps

        def add(d, raw):
            if d is None or d is I:
                return
            if (not d.dma) and (not I.dma) and d.eng == I.eng and not raw:
                return
            if (not d.dma) and (not I.dma) and d.eng == I.eng == "pe":
                return
            deps.append(d)

        for r in reads:
            r = self.R(r) if isinstance(r, str) else r
            add(r.w, True)
        for w in writes:
            w = self.R(w) if isinstance(w, str) else w
            add(w.w, False)
            for rd in w.r.values():
                add(rd, False)
        rk = ("d", I.dsem) if I.dma else I.eng
        for r in reads:
            r = self.R(r) if isinstance(r, str) else r
            r.r[rk] = I
        for w in writes:
            w = self.R(w) if isinstance(w, str) else w
            w.w = I
            w.r = {}

    def op(self, eng, fn, reads=(), writes=()):
        I = Ins(eng, fn)
        self._track(I, reads, writes)
        self.ins.append(I)
        return I

    def dma(self, q, fn, semkey, reads=(), writes=(), out=False):
        I = Ins(q, fn, dma=True)
        I.dsem = semkey
        self._track(I, reads, writes)
        self.ins.append(I)
        if out:
            self.outs.append(I)
        return I

    def finalize(self):
        nc = self.nc
        for I in self.ins:
            for d in I.deps:
                d.sig = True
        esem = {e: nc.alloc_semaphore(name="es_" + e) for e in self.E}
        cnt = {e: 0 for e in self.E}
        dcnt = {}
        waited = {e: {} for e in self.E}
        semid = {}

        def wait_on(engname, need):
            eng = self.E[engname]
            wd = waited[engname]
            for key, (sem, val) in need.items():
                if wd.get(key, 0) >= val:
                    continue
                eng.wait_ge(sem, val)
                wd[key] = val

        for I in self.ins:
            need = {}
            for d in I.deps:
                sem, val, key = d.tok
                if key not in need or need[key][1] < val:
                    need[key] = (sem, val)
            wait_on(I.eng, need)
            inst = I.fn(self.E[I.eng])
            if I.dma:
                k = I.dsem
                if k not in self.dsems:
                    self.dsems[k] = nc.alloc_semaphore(name="ds_" + k)
                    dcnt[k] = 0
                dcnt[k] += 16
                inst.then_inc(self.dsems[k], 16)
                I.tok = (self.dsems[k], dcnt[k], "d_" + k)
            elif I.sig:
                cnt[I.eng] += 1
                inst.then_inc(esem[I.eng], 1)
                I.tok = (esem[I.eng], cnt[I.eng], "e_" + I.eng)
        need = {}
        for d in self.outs:
            sem, val, key = d.tok
            if key not in need or need[key][1] < val:
                need[key] = (sem, val)
        wait_on("sp", need)
        need = {}
        for e in ("pe", "act", "dve", "pool"):
            if cnt[e] > 0:
                need["e_" + e] = (esem[e], cnt[e])
        wait_on("sp", need)
        self.counts = dict(cnt)


def fview(t, p0, npart, off, dims):
    ps = t[:].ap[0][0]
    return bass.AP(t, p0 * ps + off, [[ps, npart]] + [[s, c] for (s, c) in dims])


def build_program(nseq=2, depth=2, dbg=None):
    nc = bass.Bass("TRN2", target_bir_lowering=False)
    P = Prog(nc)
    QT = 256
    NS = QT // 128

    def din(name, shape, dt=F32):
        return nc.dram_tensor(name, list(shape), dt, kind="ExternalInput").ap()

    x_d = din("x", [nseq, SEQ, D])
    ctx_d = din("ctx", [nseq, CTX, D])
    cT_d = din("cT", [128, 8, 3])
    wmod_d = din("w_mod", [2, D, 6 * D])
    bmod_d = din("b_mod3", [2, 3, 6 * D])
    gn1_d = din("g_norm1T", [2, 128, 8])
    gn2_d = din("g_norm2T", [2, 128, 8])
    wA_d = din("wA", [2, D, WA_COLS])
    wQ_d = din("wQ", [2, D, WQ_COLS])
    wG_d = din("wG", [2, D, 3 * D])
    bg_d = din("b_gateT", [2, 128, 24])
    lam_d = din("lamb", [2, 4, 128, 64])
    gdo_d = din("g_diff_outT", [2, 128, 1])
    ggq_d = din("g_gqa_qb", [2, 128, 64])
    ggk_d = din("g_gqa_kb", [2, 128, 64])
    gmq_d = din("g_mla_qb", [2, 128, 384])
    gmkv_d = din("g_mla_kvb", [2, 128, 256])
    wuq_d = din("w_uq", [2, 384, 768])
    wukv_d = din("w_ukv2", [2, 256, 1024])
    wbr_d = [din("w_br%d" % b, [2, 512, D]) for b in range(3)]
    wout_d = din("w_out", [2, D, D])
    wfi_d = din("w_ffn_in", [2, D, 2 * FFN])
    wfo_d = din("w_ffn_out", [2, FFN, D])
    gfb_d = din("g_finalb", [128, D])
    ident_d = din("ident", [128, 128])
    sel_d = din("sel3", [3, 3, 128])
    tqc_d = din("tab_qk_cos", [128, 16, 32])
    tqs_d = din("tab_qk_sin", [128, 16, 32])
    tmc_d = din("tab_m_cos", [128, 16, 16])
    tms_d = din("tab_m_sin", [128, 16, 16])
    out_d = nc.dram_tensor("out", [nseq, SEQ, D], F32, kind="ExternalOutput").ap()
    KdT_d = nc.dram_tensor("KdT", [4, 128, NKEY], BF16, kind="Internal").ap()
    KgT_d = nc.dram_tensor("KgT", [128, NKEY], BF16, kind="Internal").ap()
    KmT_d = nc.dram_tensor("KmT", [8, 96, NKEY], BF16, kind="Internal").ap()
    Vd_d = nc.dram_tensor("Vd", [NKEY, 512], BF16, kind="Internal").ap()
    Vg_d = nc.dram_tensor("Vg", [NKEY, 128], BF16, kind="Internal").ap()
    Vm_d = nc.dram_tensor("Vm", [NKEY, 512], BF16, kind="Internal").ap()
    modg_d = nc.dram_tensor("modg_scr", [2, 2, 3, D], F32, kind="Internal").ap()
    WSRC = {"wA": wA_d, "wukv": wukv_d, "wQ": wQ_d, "wuq": wuq_d, "wG": wG_d, "wbr0": wbr_d[0], "wbr1": wbr_d[1],
            "wbr2": wbr_d[2], "wout": wout_d, "wfi": wfi_d, "wfo": wfo_d}
    WT = {
        "wA": [(0, 8, 0, 512), (0, 8, 512, 512), (0, 8, 1024, 512), (0, 8, 1536, 32)],
        "wukv": [(0, 2, 0, 512), (0, 2, 512, 512)],
        "wQ": [(0, 8, 0, 512), (0, 8, 512, 512), (0, 8, 1024, 384)],
        "wuq": [(0, 3, 0, 384), (0, 3, 384, 384)],
        "wG": [(0, 8, c * 512, 512) for c in range(6)],
        "wbr0": [(0, 4, 0, 512), (0, 4, 512, 512)], "wbr1": [(0, 4, 0, 512), (0, 4, 512, 512)],
        "wbr2": [(0, 4, 0, 512), (0, 4, 512, 512)],
        "wout": [(0, 8, 0, 512), (0, 8, 512, 512)],
        "wfi": [(0, 8, t * 512, 512 if t < 5 else 256) for t in range(6)] + [(0, 8, FFN + t * 512, 512 if t < 5 else 256) for t in range(6)],
        "wfo": [(k0, nkc, ct * 512, 512) for ct in range(2) for (k0, nkc) in ((0, 8), (8, 8), (16, 6))],
    }
    WOFF = {}
    WSCR = {}
    cast_jobs = {0: [], 1: []}
    cast_res = {}
    for nm in ("wA", "wukv", "wQ", "wuq", "wG", "wbr0", "wbr1", "wbr2", "wout", "wfi", "wfo"):
        off = 0
        for tl in WT[nm]:
            WOFF[(nm,) + tl] = off
            off += 128 * tl[1] * tl[3]
        WSCR[nm] = nc.dram_tensor(nm + "_b", [2, off], BF16, kind="Internal").ap()
    for l_ in range(2):
        for nm in ("wA", "wukv", "wQ", "wuq", "wG", "wbr0", "wbr1", "wbr2", "wout", "wfi", "wfo"):
            src_ = WSRC[nm]
            scr = WSCR[nm]
            tot = scr.shape[1]
            for tl in WT[nm]:
                k0, kc, c0, ncols = tl
                keys = []
                base = l_ * tot + WOFF[(nm,) + tl]
                for kk in range(kc):
                    if nm == "wbr1":
                        parts = [(kv_ * 64, 64, kv_ * 256 + (k0 + kk) * 64) for kv_ in range(2)]
                    else:
                        parts = [(0, 128, (k0 + kk) * 128)]
                    for (p0, npp, r0) in parts:
                        k_ = "cw_%s_%d_%d_%d_%d_%d" % (nm, l_, k0, c0, kk, p0)
                        keys.append(k_)
                        dst_ = bass.AP(scr.tensor, base + p0 * kc * ncols + kk * ncols, [[kc * ncols, npp], [1, ncols]])
                        cast_jobs[l_].append((dst_, src_[l_, r0:r0 + npp, c0:c0 + ncols], "cs_%s_%d" % (nm, l_), k_))
                cast_res.setdefault((nm, l_), []).extend(keys)

    def drain_casts(l_, n):
        for _ in range(min(n, len(cast_jobs[l_]))):
            dst_, src_, sem_, k_ = cast_jobs[l_].pop(0)
            P.dma("pool", (lambda e, dst_=dst_, src_=src_: e.dma_start(out=dst_, in_=src_)), sem_, (), [k_])

    dbg_d = None
    if dbg is not None:
        dbg_d = nc.dram_tensor("dbg", [128, dbg], F32, kind="ExternalOutput").ap()

    def sb(name, shape, dt):
        return nc.alloc_sbuf_tensor("s_" + name, shape, dt)
    x_sb = sb("x_sb", [128, 16, D], F32)
    xc_sb = sb("xc_sb", [128, 2, D], F32)
    ident_f = sb("ident_f", [128, 128], F32)
    ident_b = sb("ident_b", [128, 128], BF16)
    ones_b = sb("ones_b", [128, 128], BF16)
    tqc = sb("tqc", [128, 16, 32], F32)
    tqs = sb("tqs", [128, 16, 32], F32)
    tmc = sb("tmc", [128, 16, 16], F32)
    tms = sb("tms", [128, 16, 16], F32)
    cT = sb("cT", [128, 8, 3], F32)
    modT = sb("modT", [128, 2, 48, 3], F32)
    gsv = sb("gsv", [128, 2, 2, 8, 3], F32)
    gn = sb("gn", [128, 2, 2, 8], F32)
    bgT = sb("bgT", [128, 2, 24], F32)
    lamb = sb("lamb", [128, 4, 64], F32)
    lamv = sb("lamv", [128, 2, 4], F32)
    gdo = sb("gdo", [128, 2, 1], F32)
    ggq = sb("ggq", [128, 64], F32)
    ggk = sb("ggk", [128, 64], F32)
    gmq = sb("gmq", [128, 384], F32)
    gmkv = sb("gmkv", [128, 256], F32)
    accd = sb("accd", [128, 512], F32)
    accp = sb("accp", [128, 512], F32)
    ones_f = sb("ones_f", [128, 128], F32)
    g1b = sb("g1b", [128, D], F32)
    g2b = sb("g2b", [128, D], F32)
    NW = 3
    wring = [sb("wring%d" % i, [128, 8, 512], BF16) for i in range(NW)]
    hT = sb("hT", [128, 8, 256], BF16)
    junk = sb("junk", [128, D], BF16)
    st = sb("st", [128, 16], F32)
    diag = sb("diag", [128, 128], F32)
    zt = [sb("zt%d" % i, [128, 512], F32) for i in range(4)]
    qtm = sb("qtm", [128, 768], BF16)
    vtm = sb("vtm", [128, 512], BF16)
    kstage = sb("kstage", [128, 8, 128], BF16)
    QdT = sb("QdT", [128, 8, QT], BF16)
    QgT = sb("QgT", [128, 8, QT], BF16)
    QmT = sb("QmT", [128, 8, QT], BF16)
    cqnT = sb("cqnT", [128, 3, QT], BF16)
    kvh = sb("kvh", [128, 4 * NKEY], BF16)
    kbuf = [kvh[:, i * NKEY:(i + 1) * NKEY] for i in range(2)]
    vbuf = [kvh[:, (2 + i) * NKEY:(3 + i) * NKEY].rearrange("p (a b) -> p a b", a=18) for i in range(2)]
    hid = kvh[:, 0:22 * QT].rearrange("p (a b) -> p a b", a=22)
    HIDK = ["kbuf0", "kbuf1", "vbuf0"]
    PT = [sb("PT%d" % i, [128, 512], BF16) for i in range(4)]
    OT = [sb("OT%d" % b, [128, 4, QT], BF16) for b in range(3)]
    of32 = sb("of32", [128, QT], F32)
    sqb = sb("sqb", [128, QT], BF16)
    ysum = sb("ysum", [128, 8, QT], F32)
    yT = sb("yT", [128, 8, QT], BF16)
    outst = [ysum[:, 4 * i:4 * i + 4, :].rearrange("p a b -> p (a b)") for i in range(2)]
    sig = [sb("sig%d" % i, [128, QT], F32) for i in range(2)]
    pmm = [nc.alloc_psum_tensor("pmm%d" % i, [128, 512], F32) for i in range(3)]
    psc = [nc.alloc_psum_tensor("psc%d" % i, [128, 512], F32) for i in range(3)]
    pacc = [nc.alloc_psum_tensor("pacc%d" % i, [128, 512], F32) for i in range(2)]

    rot = {"mm": 0, "sc": 0, "w": 0, "pt": 0, "k": 0, "v": 0, "os": 0, "wm": 0, "zt": 0}

    def nxt(key, n):
        i = rot[key]
        rot[key] = (i + 1) % n
        return i

    def mm_bank():
        i = nxt("mm", 3)
        return pmm[i], "pmm%d" % i

    def dma_sp(out, in_, key, reads=(), writes=(), is_out=False):
        return P.dma("sp", lambda e: e.dma_start(out=out, in_=in_, allow_slow_non_contiguous=True), key, reads, writes, out=is_out)

    def dma_pool(out, in_, key, reads=(), writes=()):
        return P.dma("pool", lambda e: e.dma_start(out=out, in_=in_, allow_slow_non_contiguous=True), key, reads, writes)

    def act(out, in_, func, reads, writes, **kw):
        return P.op("act", lambda e: e.activation(out=out, in_=in_, func=func, **kw), reads, writes)

    def tt(out, in0, in1, op, reads, writes):
        return P.op("dve", lambda e: e.tensor_tensor(out=out, in0=in0, in1=in1, op=op), reads, writes)

    def ts(out, in0, s1, s2, op0, op1, reads, writes):
        if op1 is None:
            return P.op("dve", lambda e: e.tensor_scalar(out=out, in0=in0, scalar1=s1, scalar2=None, op0=op0), reads, writes)
        return P.op("dve", lambda e: e.tensor_scalar(out=out, in0=in0, scalar1=s1, scalar2=s2, op0=op0, op1=op1), reads, writes)

    def stt(out, in0, scalar, in1, op0, op1, reads, writes):
        return P.op("dve", lambda e: e.scalar_tensor_tensor(out=out, in0=in0, scalar=scalar, in1=in1, op0=op0, op1=op1), reads, writes)

    def cp(out, in_, reads, writes):
        return P.op("dve", lambda e: e.tensor_copy(out=out, in_=in_), reads, writes)

    def mm(out, lhsT, rhs, start, stop, reads, writes):
        return P.op("pe", lambda e: e.matmul(out, lhsT=lhsT, rhs=rhs, start=start, stop=stop), reads, writes)

    def tp(out, in_, reads, writes, f32=False):
        idn = ident_f if f32 else ident_b
        np_ = in_.shape[0]
        return P.op("pe", lambda e: e.transpose(out, in_, idn[0:np_, 0:np_]), list(reads) + ["ident"], writes)

    def wtile(nm, l_, k0, kc, c0, ncols):
        i = nxt("w", NW)
        t = wring[i]
        scr = WSCR[nm]
        base = l_ * scr.shape[1] + WOFF[(nm, k0, kc, c0, ncols)]
        src = bass.AP(scr.tensor, base, [[kc * ncols, 128], [ncols, kc], [1, ncols]])
        dma_pool(t[:, 0:kc, 0:ncols], src, "w%d" % i, cast_res[(nm, l_)], ["w%d" % i])
        return t, "w%d" % i

    def rsqrt_small(ap, n, key):
        act(ap, ap, AF.Ln, [key], [key])
        act(ap, ap, AF.Exp, [key], [key], scale=-0.5)

    dma_sp(ident_f[:], ident_d, "c0", (), ["ident"])
    dma_pool(ident_b[:], ident_d, "c1", (), ["ident"])
    P.op("dve", lambda e: e.memset(ones_b[:], 1.0), (), ["ones"])
    P.op("dve", lambda e: e.memset(ones_f[:], 1.0), (), ["ones_f"])
    P.op("dve", lambda e: e.memset(QdT[:], 0.0), (), ["QdT"])
    P.op("dve", lambda e: e.memset(QgT[:], 0.0), (), ["QgT"])
    P.op("dve", lambda e: e.memset(QmT[:], 0.0), (), ["QmT"])
    dma_sp(tqc[:], tqc_d, "c3", (), ["tab"])
    dma_sp(tqs[:], tqs_d, "c4", (), ["tab"])
    dma_sp(tmc[:], tmc_d, "c5", (), ["tab"])
    dma_sp(tms[:], tms_d, "c6", (), ["tab"])
    dma_sp(cT[:], cT_d, "c8", (), ["cT"])
    dma_sp(gn[:, :, 0, :], gn1_d.rearrange("l p c -> p l c"), "c9", (), ["gn"])
    dma_sp(gn[:, :, 1, :], gn2_d.rearrange("l p c -> p l c"), "c10", (), ["gn"])
    dma_sp(bgT[:], bg_d.rearrange("l p c -> p l c"), "c11", (), ["bgT"])
    dma_sp(gdo[:], gdo_d.rearrange("l p c -> p l c"), "c12", (), ["gdo"])
    act(cT[:], cT[:], AF.Silu, ["cT"], ["cT"])

    lam_init = [0.8 - 0.6 * math.exp(-0.3 * l) for l in range(2)]
    for l in range(depth):
        dma_sp(lamb[:], lam_d[l].rearrange("k p c -> p k c"), "c17", (), ["lamb"])
        tt(zt[0][:, 0:64], lamb[:, 0, :], lamb[:, 1, :], ALU.mult, ["lamb"], ["zt0"])
        tt(zt[0][:, 64:128], lamb[:, 2, :], lamb[:, 3, :], ALU.mult, ["lamb"], ["zt0"])
        P.op("dve", lambda e: e.tensor_reduce(out=st[:, 0:2], in_=zt[0][:, 0:128].rearrange("p (a b) -> p a b", a=2), axis=AX.X, op=ALU.add), ["zt0"], ["st"])
        act(st[:, 0:2], st[:, 0:2], AF.Exp, ["st"], ["st"])
        tt(st[:, 2:3], st[:, 1:2], st[:, 0:1], ALU.subtract, ["st"], ["st"])
        _l = l
        ts(lamv[:, l, 0:1], st[:, 2:3], -lam_init[l], None, ALU.add, None, ["st"], ["lamv"])
        ts(gdo[:, l, :], gdo[:, l, :], 1.0 - lam_init[l], None, ALU.mult, None, ["gdo"], ["gdo"])
        for cc in range(24):
            i = nxt("w", NW)
            wt = wring[i][:].rearrange("p a b -> p (a b)").bitcast(F32).rearrange("p (a b) -> p a b", a=8)
            wmk = "w%d" % i
            dma_sp(wt, wmod_d[l, :, cc * 256:(cc + 1) * 256].rearrange("(kc p) n -> p kc n", p=128), wmk, (), [wmk])
            bmt = zt[1]
            dma_sp(bmt[0:3, 0:256], bmod_d[l, :, cc * 256:(cc + 1) * 256], "bm", (), ["zt1"])
            ps, pk = mm_bank()
            for kc in range(8):
                mm(ps[0:3, 0:256], cT[:, kc, :], wt[:, kc, :], kc == 0, kc == 7, ["cT", wmk], [pk])
            tt(bmt[0:3, 0:256], ps[0:3, 0:256], bmt[0:3, 0:256], ALU.add, [pk, "zt1"], ["zt1"])
            which = cc // 4
            if which in (2, 5):
                gi = 0 if which == 2 else 1
                dma_sp(modg_d[l, gi, :, (cc % 4) * 256:(cc % 4) * 256 + 256], bmt[0:3, 0:256], "mg", ["zt1"], ["modg%d" % l])
            else:
                ps2, pk2 = mm_bank()
                for q in range(2):
                    mm(ps2[:, q * 3:(q + 1) * 3], bmt[0:3, q * 128:(q + 1) * 128], ident_f[0:3, 0:3], True, True, ["zt1", "ident"], [pk2])
                cp(modT[:, l, cc * 2:(cc + 1) * 2, :], ps2[:, 0:6].rearrange("p (a b) -> p a b", a=2), [pk2], ["modT"])
        for ni, wh in ((0, 1), (1, 4)):
            for s in range(3):
                stt(gsv[:, l, ni, :, s], modT[:, l, wh * 8:(wh + 1) * 8, s], 1.0, gn[:, l, ni, :], ALU.add, ALU.mult, ["modT", "gn"], ["gsv"])

    def bcast_gates(l, s):
        for gi, gt, gk in ((0, g1b, "g1b"), (1, g2b, "g2b")):
            src = modg_d[l, gi, s:s + 1, :]
            srcb = bass.AP(src.tensor, src.offset, [[0, 128], [1, D]])
            dma_sp(gt[:], srcb, "gb%d" % gi, ["modg%d" % l], [gk])

    def rms_hT(xap, xres, l, ni, s, col0):
        act(junk[:], xap, AF.Square, [xres], ["junk", "st4"], accum_out=st[:, 4:5])
        ts(st[:, 5:6], st[:, 4:5], 1.0 / D, EPS, ALU.mult, ALU.add, ["st4"], ["st5"])
        rsqrt_small(st[:, 5:6], 1, "st5")
        ts(diag[:], ident_f[:], st[:, 5:6], None, ALU.mult, None, ["ident", "st5"], ["diag"])
        banks = [mm_bank(), mm_bank()]
        for c in range(8):
            ps, pk = banks[c // 4]
            mm(ps[:, (c % 4) * 128:(c % 4 + 1) * 128], xap[:, c * 128:(c + 1) * 128], diag[:], True, True, [xres, "diag"], [pk])
        wh = 0 if ni == 0 else 3
        for c in range(8):
            ps, pk = banks[c // 4]
            act(hT[:, c, col0:col0 + 128], ps[:, (c % 4) * 128:(c % 4 + 1) * 128], AF.Identity, [pk, "gsv", "modT"], ["hT"],
                scale=gsv[:, l, ni, c, s:s + 1], bias=modT[:, l, wh * 8 + c, s:s + 1])

    def group_norm(src, ncol, G, gvec, dst_f32, key):
        gs = ncol // G
        sq = zt[3]
        act(sq[:, 0:ncol], src, AF.Square, [key], ["zt3"])
        P.op("dve", lambda e: e.tensor_reduce(out=st[:, 8:8 + G], in_=sq[:, 0:ncol].rearrange("p (g d) -> p g d", g=G), axis=AX.X, op=ALU.add), ["zt3"], ["st8"])
        ts(st[:, 8:8 + G], st[:, 8:8 + G], 1.0 / gs, EPS, ALU.mult, ALU.add, ["st8"], ["st8"])
        rsqrt_small(st[:, 8:8 + G], G, "st8")
        srcv = src.rearrange("p (g d) -> p g d", g=G)
        dstv = dst_f32[:, 0:ncol].rearrange("p (g d) -> p g d", g=G)
        rb = fview(st, 0, 128, 8, [(1, G), (0, gs)])
        tt(dstv, srcv, rb, ALU.mult, [key, "st8"], ["_gn_dst"])
        gb = bass.AP(gvec.tensor, gvec.offset, [list(gvec.ap[0]), [0, G], [1, gs]])
        tt(dstv, dstv, gb, ALU.mult, ["_gn_dst", "gvec"], ["_gn_dst"])

    def rope(src, srckey, H, hd, dst_bf, dstkey, cosap, sinap, nfreq, dst_hstride=None, dst_off=0, src_hstride=None, src_off=0):
        sh = src_hstride or hd
        dh = dst_hstride or hd

        def sv(t, off, hs, half):
            return fview(t, 0, 128, off + half * nfreq, [(hs, H), (2 * nfreq, 2), (1, nfreq)])

        cb = bass.AP(cosap.tensor, cosap.offset, [list(cosap.ap[0]), [0, H], [nfreq, 2], [1, nfreq]])
        sbp = bass.AP(sinap.tensor, sinap.offset, [list(sinap.ap[0]), [0, H], [nfreq, 2], [1, nfreq]])
        n = H * 2 * nfreq
        ta = zt[1]
        tb = zt[2]
        tav = fview(ta, 0, 128, 0, [(2 * nfreq, H), (nfreq, 2), (1, nfreq)])
        tbv = fview(tb, 0, 128, 0, [(2 * nfreq, H), (nfreq, 2), (1, nfreq)])
        x1 = sv(src, src_off, sh, 0)
        x2 = sv(src, src_off, sh, 1)
        tt(tav, x1, cb, ALU.mult, [srckey, "tab"], ["zt1"])
        tt(tbv, x2, sbp, ALU.mult, [srckey, "tab"], ["zt2"])
        tt(sv(dst_bf, dst_off, dh, 0), tav, tbv, ALU.subtract, ["zt1", "zt2"], [dstkey])
        tt(tav, x1, sbp, ALU.mult, [srckey, "tab"], ["zt1"])
        tt(tbv, x2, cb, ALU.mult, [srckey, "tab"], ["zt2"])
        tt(sv(dst_bf, dst_off, dh, 1), tav, tbv, ALU.add, ["zt1", "zt2"], [dstkey])

    def tp_bank():
        ps, pk = mm_bank()
        return ps[:].bitcast(BF16), pk

    def transposes_to(src_bf, srckey, blocks, dst_fn, dstkey):
        for i0 in range(0, len(blocks), 8):
            grp = blocks[i0:i0 + 8]
            ptp, ptk = tp_bank()
            for j, (c0, ncol) in enumerate(grp):
                tp(ptp[0:ncol, j * 128:(j + 1) * 128], src_bf[:, c0:c0 + ncol], [srckey], [ptk])
            ncol = grp[0][1]
            dst = dst_fn(i0, len(grp))
            cp(dst, ptp[0:ncol, 0:len(grp) * 128].rearrange("p (a b) -> p a b", a=len(grp)), [ptk], [dstkey])

    def transposes_pad(src_bf, srckey, Qp, qk, i):
        ptp, ptk = tp_bank()
        for j in range(4):
            tp(ptp[:, j * 128:(j + 1) * 128], src_bf[:, j * 128:(j + 1) * 128], [srckey], [ptk])
        QTn = Qp.shape[2]
        cp(fview(Qp, 0, 64, i * 128, [(2 * QTn, 4), (1, 128)]), ptp[0:64, 0:512].rearrange("p (a b) -> p a b", a=4), [ptk], [qk])
        cp(fview(Qp, 64, 64, QTn + i * 128, [(2 * QTn, 4), (1, 128)]), ptp[64:128, 0:512].rearrange("p (a b) -> p a b", a=4), [ptk], [qk])

    ckvnTa = sb("ckvnTa", [128, 2, 256], BF16)
    krtma = sb("krtma", [128, 2, 32], BF16)
    kmtm = sb("kmtm", [128, 768], BF16)
    kst2 = [kstage, sb("kstage1", [128, 8, 128], BF16)]
    vtm2 = [vtm, sb("vtm1", [128, 512], BF16)]

    def kst():
        i = nxt("k", 2)
        return kst2[i], "kstage%d" % i, "kst%d" % i

    def vst():
        i = nxt("v", 2)
        return vtm2[i], "vtm%d" % i, "vst%d" % i

    def phase_a_tile(l, xbuf, xkey, sub0, nsub, key0, latent, lat_sub0, mslot):
        for i in range(nsub):
            rms_hT(xbuf[:, sub0 + i, :], "%s%d" % (xkey, sub0 + i), l, 0, mslot, i * 128)
        groups = [(0, 512), (512, 512), (1024, 512), (1536, 32)]
        for gi, (c0, ncol) in enumerate(groups):
            wt, wk = wtile("wA", l, 0, 8, c0, ncol)
            for i in range(nsub):
                ks = key0 + i * 128
                kt = ks // 128
                ps, pk = mm_bank()
                for kc in range(8):
                    mm(ps[:, 0:ncol], hT[:, kc, i * 128:(i + 1) * 128], wt[:, kc, 0:ncol], kc == 0, kc == 7, ["hT", wk], [pk])
                tsub = lat_sub0 + i
                if gi == 0:
                    if latent:
                        rope(ps, pk, 8, 64, qtm, "qtm", tqc[:, tsub, :], tqs[:, tsub, :], 16)
                    else:
                        cp(qtm[:, 0:512], ps[:], [pk], ["qtm"])
                    kb_, kbk, ksem = kst()
                    transposes_to(qtm, "qtm", [(j * 128, 128) for j in range(4)], lambda i0, n: kb_[:, 0:4, :], kbk)
                    dma_sp(KdT_d[:, :, ks:ks + 128].rearrange("j p n -> p j n"), kb_[:, 0:4, :], ksem, [kbk], ["KdT_%d" % kt])
                elif gi == 1:
                    vb_, vbk, vsem = vst()
                    cp(vb_[:], ps[:], [pk], [vbk])
                    dma_sp(Vd_d[ks:ks + 128, :], vb_[:], vsem, [vbk], ["Vd_%d" % kt])
                elif gi == 2:
                    group_norm(ps[:, 0:128], 128, 2, ggk[:], zt[0], pk)
                    if latent:
                        rope(zt[0], "_gn_dst", 2, 64, qtm, "qtm", tqc[:, tsub, :], tqs[:, tsub, :], 16)
                    else:
                        cp(qtm[:, 0:128], zt[0][:, 0:128], ["_gn_dst"], ["qtm"])
                    kb_, kbk, ksem = kst()
                    transposes_to(qtm, "qtm", [(0, 128)], lambda i0, n: kb_[:, 0:1, :], kbk)
                    dma_sp(KgT_d[:, ks:ks + 128], kb_[:, 0, :], ksem, [kbk], ["KgT_%d" % kt])
                    vb_, vbk, vsem = vst()
                    cp(vb_[:, 0:128], ps[:, 128:256], [pk], [vbk])
                    dma_sp(Vg_d[ks:ks + 128, :], vb_[:, 0:128], vsem, [vbk], ["Vg_%d" % kt])
                    group_norm(ps[:, 256:512], 256, 1, gmkv[:], zt[0], pk)
                    cp(qtm[:, 0:256], zt[0][:, 0:256], ["_gn_dst"], ["qtm"])
                    transposes_to(qtm, "qtm", [(0, 128), (128, 128)], lambda i0, n: ckvnTa[:, :, i * 128:(i + 1) * 128], "ckvnTa")
                else:
                    if latent:
                        rope(ps, pk, 1, 32, krtma, "krtma", tmc[:, tsub, :], tms[:, tsub, :], 8, dst_off=i * 32)
                    else:
                        cp(krtma[:, i, :], ps[:, 0:32], [pk], ["krtma"])
        wkt, wkk = wtile("wukv", l, 0, 2, 0, 512)
        wvt, wvk = wtile("wukv", l, 0, 2, 512, 512)
        for i in range(nsub):
            ks = key0 + i * 128
            kt = ks // 128
            ps, pk = mm_bank()
            for kc in range(2):
                mm(ps[:], ckvnTa[:, kc, i * 128:(i + 1) * 128], wkt[:, kc, :], kc == 0, kc == 1, ["ckvnTa", wkk], [pk])
            cp(fview(kmtm, 0, 128, 0, [(96, 8), (1, 64)]), ps[:].rearrange("p (h d) -> p h d", h=8), [pk], ["kmtm"])
            cp(fview(kmtm, 0, 128, 64, [(96, 8), (1, 32)]), fview(krtma, 0, 128, i * 32, [(0, 8), (1, 32)]), ["krtma"], ["kmtm"])
            kb_, kbk, ksem = kst()
            transposes_to(kmtm, "kmtm", [(h * 96, 96) for h in range(8)], lambda i0, n: kb_[0:96, :, :], kbk)
            dma_sp(KmT_d[:, :, ks:ks + 128].rearrange("h p n -> p h n"), kb_[0:96, :, :], ksem, [kbk], ["KmT_%d" % kt])
            ps2, pk2 = mm_bank()
            for kc in range(2):
                mm(ps2[:], ckvnTa[:, kc, i * 128:(i + 1) * 128], wvt[:, kc, :], kc == 0, kc == 1, ["ckvnTa", wvk], [pk2])
            vb_, vbk, vsem = vst()
            cp(vb_[:], ps2[:], [pk2], [vbk])
            dma_sp(Vm_d[ks:ks + 128, :], vb_[:], vsem, [vbk], ["Vm_%d" % kt])

    def phase_b_tile(l, xbuf, xkey, sub0, nsub, latent, lat_sub0, mslot, nk):
        QN = nsub * 128
        xk = ["%s%d" % (xkey, sub0 + i) for i in range(nsub)]
        for i in range(nsub):
            rms_hT(xbuf[:, sub0 + i, :], xk[i], l, 0, mslot, i * 128)
        for gi, (c0, ncol) in enumerate([(0, 512), (512, 512), (1024, 384)]):
            wt, wk = wtile("wQ", l, 0, 8, c0, ncol)
            for i in range(nsub):
                ps, pk = mm_bank()
                for kc in range(8):
                    mm(ps[:, 0:ncol], hT[:, kc, i * 128:(i + 1) * 128], wt[:, kc, 0:ncol], kc == 0, kc == 7, ["hT", wk], [pk])
                tsub = lat_sub0 + i
                if gi == 0:
                    if latent:
                        rope(ps, pk, 8, 64, qtm, "qtm", tqc[:, tsub, :], tqs[:, tsub, :], 16)
                    else:
                        cp(qtm[:, 0:512], ps[:], [pk], ["qtm"])
                    transposes_pad(qtm, "qtm", QdT, "QdT", i)
                elif gi == 1:
                    group_norm(ps[:, 0:512], 512, 8, ggq[:], zt[0], pk)
                    for kv in range(2):
                        if latent:
                            rope(zt[0], "_gn_dst", 4, 64, qtm, "qtm", tqc[:, tsub, :], tqs[:, tsub, :], 16,
                                 dst_hstride=128, dst_off=kv * 64, src_hstride=64, src_off=kv * 256)
                        else:
                            cp(fview(qtm, 0, 128, kv * 64, [(128, 4), (1, 64)]), fview(zt[0], 0, 128, kv * 256, [(64, 4), (1, 64)]), ["_gn_dst"], ["qtm"])
                    transposes_pad(qtm, "qtm", QgT, "QgT", i)
                else:
                    group_norm(ps[:, 0:384], 384, 1, gmq[:], zt[0], pk)
                    cp(qtm[:, 0:384], zt[0][:, 0:384], ["_gn_dst"], ["qtm"])
                    transposes_to(qtm, "qtm", [(j * 128, 128) for j in range(3)], lambda i0, n: cqnT[:, 0:3, i * 128:(i + 1) * 128], "cqnT")
        for hh in range(2):
            wt, wk = wtile("wuq", l, 0, 3, hh * 384, 384)
            for i in range(nsub):
                tsub = lat_sub0 + i
                ps, pk = mm_bank()
                for kc in range(3):
                    mm(ps[:, 0:384], cqnT[:, kc, i * 128:(i + 1) * 128], wt[:, kc, 0:384], kc == 0, kc == 2, ["cqnT", wk], [pk])
                cp(fview(qtm, 0, 128, 0, [(96, 4), (1, 64)]), fview(ps, 0, 128, 0, [(96, 4), (1, 64)]), [pk], ["qtm"])
                if latent:
                    rope(ps, pk, 4, 32, qtm, "qtm", tmc[:, tsub, :], tms[:, tsub, :], 8, dst_hstride=96, dst_off=64, src_hstride=96, src_off=64)
                else:
                    cp(fview(qtm, 0, 128, 64, [(96, 4), (1, 32)]), fview(ps, 0, 128, 64, [(96, 4), (1, 32)]), [pk], ["qtm"])
                transposes_to(qtm, "qtm", [(h * 96, 96) for h in range(4)], lambda i0, n: QmT[0:96, hh * 4:(hh + 1) * 4, i * 128:(i + 1) * 128], "QmT")

        nkt = nk

        def kv_reads(name):
            return ["%s_%d" % (name, t) for t in range(nkt)]

        def load_k(src, npart, rd):
            i = nxt("k2", 2)
            kb_ = kbuf[i]
            dma_sp(kb_[0:npart, 0:nk * 128], src, "kb%d" % i, kv_reads(rd), ["kbuf%d" % i])
            return kb_, "kbuf%d" % i

        def load_v(src, rd):
            i = nxt("v2", 2)
            vb_ = vbuf[i]
            dma_sp(vb_[:, 0:nk, :], src.rearrange("(kc p) d -> p kc d", p=128), "vb%d" % i, kv_reads(rd), ["vbuf%d" % i])
            return vb_, "vbuf%d" % i

        sc_d = 64 ** -0.5
        sc_m = 96 ** -0.5
        steps = []

        def add_head(kget, vget, qap, qkey, co, scale, fin):
            for kc0 in range(0, nk, 2):
                n2 = min(2, nk - kc0)

                def score(kc0=kc0, n2=n2):
                    kb_, kbk = kget()
                    vget()
                    si = nxt("sc", 3)
                    ps = psc[si]
                    pk = "psc%d" % si
                    for u in range(n2):
                        kc = kc0 + u
                        mm(ps[:, u * QN:(u + 1) * QN], kb_[:, kc * 128:(kc + 1) * 128], qap, True, True, [kbk, qkey], [pk])
                    pi = nxt("pt", 4)
                    pt = PT[pi]
                    ptk = "PT%d" % pi
                    act(pt[:, 0:n2 * QN], ps[:, 0:n2 * QN], AF.Exp, [pk], [ptk], scale=scale)
                    return pt, ptk

                sidx = kc0 // 2
                on_pool = (sidx % 3 == 2)
                first_d = (sidx == 0)
                first_p = (sidx == 2)
                last = (kc0 + 2 >= nk)
                use_p = (nk > 4)

                def pv(st_, kc0=kc0, n2=n2, on_pool=on_pool, first_d=first_d, first_p=first_p, last=last, use_p=use_p):
                    pt, ptk = st_
                    vb_, vbk = vget()
                    for u in range(n2):
                        kc = kc0 + u
                        mm(pacc[0][:, co:co + QN], vb_[:, kc, :], pt[:, u * QN:(u + 1) * QN], kc == 0, kc == nk - 1, [vbk, ptk], ["pacc0"])
                    w = n2 * QN
                    if on_pool:
                        if first_p:
                            P.op("pool", lambda e: e.tensor_copy(out=accp[:, 0:w], in_=pt[:, 0:w]), [ptk], ["accp"])
                        else:
                            P.op("pool", lambda e: e.tensor_tensor(out=accp[:, 0:w], in0=accp[:, 0:w], in1=pt[:, 0:w], op=ALU.add), [ptk, "accp"], ["accp"])
                    else:
                        if first_d:
                            if w < 2 * QN:
                                P.op("dve", lambda e: e.memset(accd[:], 0.0), (), ["accd"])
                            cp(accd[:, 0:w], pt[:, 0:w], [ptk], ["accd"])
                        else:
                            tt(accd[:, 0:w], accd[:, 0:w], pt[:, 0:w], ALU.add, [ptk, "accd"], ["accd"])
                    if last:
                        srcs = [(accd, "accd", 0), (accd, "accd", QN)]
                        if use_p:
                            srcs += [(accp, "accp", 0), (accp, "accp", QN)]
                        for si_, (a_, ak_, o_) in enumerate(srcs):
                            mm(pacc[1][:, co:co + QN], ones_f[:], a_[:, o_:o_ + QN], si_ == 0, si_ == len(srcs) - 1, ["ones_f", ak_], ["pacc1"])
                steps.append((score, pv, fin if last else None))

        def lazy(fn):
            box = []

            def get():
                if not box:
                    box.append(fn())
                return box[0]
            return get

        def cpa(out, in_, reads, writes):
            return P.op("act", lambda e: e.activation(out=out, in_=in_, func=AF.Copy), reads, writes)

        def fin_pair(OTt, otk, jj):
            def f():
                cpa(zt[1][0:64, 0:QN], pacc[1][0:64, 0:QN], ["pacc1"], ["zt1"])
                cpa(zt[1][64:128, 0:QN], pacc[1][64:128, QN:2 * QN], ["pacc1"], ["zt1"])
                cpa(zt[0][0:64, 0:QN], pacc[0][0:64, 0:QN], ["pacc0"], ["zt0"])
                cpa(zt[0][64:128, 0:QN], pacc[0][64:128, QN:2 * QN], ["pacc0"], ["zt0"])
                P.op("dve", lambda e: e.reciprocal(out=zt[1][:, 0:QN], in_=zt[1][:, 0:QN]), ["zt1"], ["zt1"])
                tt(OTt[:, jj, 0:QN], zt[0][:, 0:QN], zt[1][:, 0:QN], ALU.mult, ["zt0", "zt1"], [otk])
            return f

        def fin_diff(h):
            def f():
                cpa(zt[1][:, 0:2 * QN], pacc[1][:, 0:2 * QN], ["pacc1"], ["zt1"])
                cpa(zt[0][:, 0:2 * QN], pacc[0][:, 0:2 * QN], ["pacc0"], ["zt0"])
                P.op("dve", lambda e: e.reciprocal(out=zt[1][:, 0:2 * QN], in_=zt[1][:, 0:2 * QN]), ["zt1"], ["zt1"])
                tt(zt[0][:, 0:2 * QN], zt[0][:, 0:2 * QN], zt[1][:, 0:2 * QN], ALU.mult, ["zt0", "zt1"], ["zt0"])
                stt(of32[:, 0:QN], zt[0][:, QN:2 * QN], lamv[:, l, 0:1], zt[0][:, 0:QN], ALU.mult, ALU.add, ["zt0", "lamv"], ["of32"])
                act(sqb[:, 0:QN], of32[:, 0:QN], AF.Square, ["of32"], ["sqb"])
                ps, pk = mm_bank()
                mm(ps[:, 0:QN], ones_b[:, 0:128], sqb[:, 0:QN], True, True, ["ones", "sqb"], [pk])
                ts(zt[2][:, 0:QN], ps[:, 0:QN], 1.0 / 128, EPS, ALU.mult, ALU.add, [pk], ["zt2"])
                rsqrt_small(zt[2][:, 0:QN], QN, "zt2")
                stt(OT[0][:, h, 0:QN], of32[:, 0:QN], gdo[:, l, 0:1], zt[2][:, 0:QN], ALU.mult, ALU.mult, ["of32", "gdo", "zt2"], ["OT0"])
            return f

        for h in range(4):
            kget = lazy(lambda h=h: load_k(KdT_d[h, :, 0:nk * 128], 128, "KdT"))
            vget = lazy(lambda h=h: load_v(Vd_d[0:nk * 128, h * 128:(h + 1) * 128], "Vd"))
            for m in range(2):
                add_head(kget, vget, QdT[:, h * 2 + m, 0:QN], "QdT", m * QN, sc_d, fin_diff(h) if m == 1 else None)
        kget_g = lazy(lambda: load_k(KgT_d[:, 0:nk * 128], 128, "KgT"))
        vget_g = lazy(lambda: load_v(Vg_d[0:nk * 128, :], "Vg"))
        for j in range(4):
            for kv in range(2):
                add_head(kget_g, vget_g, QgT[:, j * 2 + kv, 0:QN], "QgT", kv * QN, sc_d, fin_pair(OT[1], "OT1", j) if kv == 1 else None)
        for jj in range(4):
            vget = lazy(lambda jj=jj: load_v(Vm_d[0:nk * 128, jj * 128:(jj + 1) * 128], "Vm"))
            for h in (2 * jj, 2 * jj + 1):
                kget = lazy(lambda h=h: load_k(KmT_d[h, :, 0:nk * 128], 96, "KmT"))
                add_head(kget, vget, QmT[:, h, 0:QN], "QmT", (h % 2) * QN, sc_m, fin_pair(OT[2], "OT2", jj) if h % 2 == 1 else None)
        DEPTH_P = 2
        pend = []
        for (score, pv, fin) in steps:
            st_ = score()
            pend.append((pv, fin, st_))
            if len(pend) > DEPTH_P:
                pv0, fin0, st0 = pend.pop(0)
                pv0(st0)
                if fin0 is not None:
                    fin0()
        while pend:
            pv0, fin0, st0 = pend.pop(0)
            pv0(st0)
            if fin0 is not None:
                fin0()
        for b in range(3):
            for jj in range(2):
                wg, wgk = wtile("wG", l, 0, 8, b * 1024 + jj * 512, 512)
                wb_, wbk = wtile("wbr%d" % b, l, 0, 4, jj * 512, 512)
                for j4 in range(4):
                    j = jj * 4 + j4
                    psg, pkg = mm_bank()
                    for kc in range(8):
                        mm(psg[:, 0:QN], wg[:, kc, j4 * 128:(j4 + 1) * 128], hT[:, kc, 0:QN], kc == 0, kc == 7, [wgk, "hT"], [pkg])
                    psb, pkb = mm_bank()
                    for kc in range(4):
                        mm(psb[:, 0:QN], wb_[:, kc, j4 * 128:(j4 + 1) * 128], OT[b][:, kc, 0:QN], kc == 0, kc == 3, [wbk, "OT%d" % b], [pkb])
                    si = nxt("sg", 2)
                    sg = sig[si]
                    sgk = "sig%d" % si
                    act(sg[:, 0:QN], psg[:, 0:QN], AF.Sigmoid, [pkg, "bgT"], [sgk], bias=bgT[:, l, b * 8 + j:b * 8 + j + 1])
                    yk = "ysum%d" % j
                    if b == 0:
                        tt(ysum[:, j, 0:QN], sg[:, 0:QN], psb[:, 0:QN], ALU.mult, [sgk, pkb], [yk])
                    elif b == 1:
                        tt(sg[:, 0:QN], sg[:, 0:QN], psb[:, 0:QN], ALU.mult, [sgk, pkb], [sgk])
                        tt(ysum[:, j, 0:QN], ysum[:, j, 0:QN], sg[:, 0:QN], ALU.add, [yk, sgk], [yk])
                    else:
                        tt(sg[:, 0:QN], sg[:, 0:QN], psb[:, 0:QN], ALU.mult, [sgk, pkb], [sgk])
                        tt(yT[:, j, 0:QN], ysum[:, j, 0:QN], sg[:, 0:QN], ALU.add, [yk, sgk], ["yT"])
        for ct in range(2):
            wt, wk = wtile("wout", l, 0, 8, ct * 512, 512)
            for i in range(nsub):
                ps, pk = mm_bank()
                for kc in range(8):
                    mm(ps[:], yT[:, kc, i * 128:(i + 1) * 128], wt[:, kc, :], kc == 0, kc == 7, ["yT", wk], [pk])
                tt(zt[0][:], ps[:], g1b[:, ct * 512:(ct + 1) * 512], ALU.mult, [pk, "g1b"], ["zt0"])
                xs = xbuf[:, sub0 + i, ct * 512:(ct + 1) * 512]
                tt(xs, xs, zt[0][:], ALU.add, [xk[i], "zt0"], [xk[i]])
        for i in range(nsub):
            rms_hT(xbuf[:, sub0 + i, :], xk[i], l, 1, mslot, i * 128)
        for t in range(6):
            ncol = 512 if t < 5 else 256
            wgt, wgk = wtile("wfi", l, 0, 8, t * 512, ncol)
            wut, wuk = wtile("wfi", l, 0, 8, FFN + t * 512, ncol)
            for j4 in range(ncol // 128):
                j = t * 4 + j4
                psg, pkg = mm_bank()
                for kc in range(8):
                    mm(psg[:, 0:QN], wgt[:, kc, j4 * 128:(j4 + 1) * 128], hT[:, kc, 0:QN], kc == 0, kc == 7, [wgk, "hT"], [pkg])
                psu, pku = mm_bank()
                for kc in range(8):
                    mm(psu[:, 0:QN], wut[:, kc, j4 * 128:(j4 + 1) * 128], hT[:, kc, 0:QN], kc == 0, kc == 7, [wuk, "hT"], [pku])
                si = nxt("sg", 2)
                sg = sig[si]
                sgk = "sig%d" % si
                act(sg[:, 0:QN], psg[:, 0:QN], AF.Silu, [pkg], [sgk])
                tt(hid[:, j, 0:QN], sg[:, 0:QN], psu[:, 0:QN], ALU.mult, [sgk, pku], HIDK)
        for ct in range(2):
            for (k0, nkc) in ((0, 8), (8, 8), (16, 6)):
                wt, wk = wtile("wfo", l, k0, nkc, ct * 512, 512)
                for i in range(nsub):
                    for kk in range(nkc):
                        kc = k0 + kk
                        mm(pacc[i][:], hid[:, kc, i * 128:(i + 1) * 128], wt[:, kk, :], kc == 0, kc == 21, HIDK + [wk], ["pacc%d" % i])
            for i in range(nsub):
                tt(zt[0][:], pacc[i][:], g2b[:, ct * 512:(ct + 1) * 512], ALU.mult, ["pacc%d" % i, "g2b"], ["zt0"])
                xs = xbuf[:, sub0 + i, ct * 512:(ct + 1) * 512]
                tt(xs, xs, zt[0][:], ALU.add, [xk[i], "zt0"], [xk[i]])

    rot.update({"k2": 0, "v2": 0, "sg": 0})

    for s in range(nseq):
        for t in range(16):
            dma_sp(x_sb[:, t, :], x_d[s, t * 128:(t + 1) * 128, :], "xl", [], ["x%d" % t])
        for t in range(2):
            dma_sp(xc_sb[:, t, :], ctx_d[s, t * 128:(t + 1) * 128, :], "xl", [], ["xc%d" % t])
        _last = P.ins[-1]
        for t in range(16):
            P.R("x%d" % t).w = _last
        for t in range(2):
            P.R("xc%d" % t).w = _last
        for l in range(depth):
            first = (s == 0)
            dma_sp(ggq[:], ggq_d[l], "c13", (), ["gvec"])
            dma_sp(ggk[:], ggk_d[l], "c14", (), ["gvec"])
            dma_sp(gmq[:], gmq_d[l], "c15", (), ["gvec"])
            dma_sp(gmkv[:], gmkv_d[l], "c16", (), ["gvec"])
            if first and l == 0:
                drain_casts(0, 36)
            phase_a_tile(l, xc_sb, "xc", 0, 2, 0, False, 0, 2)
            for q in range(8):
                if first and l == 0:
                    drain_casts(0, 35)
                phase_a_tile(l, x_sb, "x", q * 2, 2, CTX + q * 256, True, q * 2, s)
            if first and l == 0:
                drain_casts(0, 1000)
            if l < depth - 1:
                bcast_gates(l, 2)
                phase_b_tile(l, xc_sb, "xc", 0, 2, False, 0, 2, 2)
            bcast_gates(l, s)
            for q in range(SEQ // QT):
                phase_b_tile(l, x_sb, "x", q * NS, NS, True, q * NS, s, NKEY // 128)
                if first and l == 0 and depth > 1:
                    drain_casts(1, 40 if q < 7 else 1000)
        dma_sp(g2b[:], gfb_d, "c7", (), ["g2b"])
        for t in range(16):
            act(junk[:], x_sb[:, t, :], AF.Square, ["x%d" % t], ["junk", "st4"], accum_out=st[:, 4:5])
            ts(st[:, 5:6], st[:, 4:5], 1.0 / D, EPS, ALU.mult, ALU.add, ["st4"], ["st5"])
            rsqrt_small(st[:, 5:6], 1, "st5")
            oi = nxt("os", 2)
            ok_ = ["ysum%d" % (4 * oi + q_) for q_ in range(4)]
            stt(outst[oi], x_sb[:, t, :], st[:, 5:6], g2b[:], ALU.mult, ALU.mult, ["x%d" % t, "st5", "g2b"], ok_)
            dma_sp(out_d[s, t * 128:(t + 1) * 128, :], outst[oi], "o%d" % oi, ok_, [], is_out=True)
    P.finalize()
    return nc, P


_IN_SPLITS = np.cumsum([512, 512, 512, 512, 128, 128, 384, 256, 32, 3072])


def _tables():
    t = np.arange(SEQ)
    row = (t // 64).astype(np.float32)
    col = (t % 64).astype(np.float32)

    def tab(dim):
        a = dim // 2
        fr = (THETA ** (-np.arange(0, a, 2, dtype=np.float32) / a)).astype(np.float32)
        ar = row[:, None] * fr
        ac = col[:, None] * fr
        c = np.concatenate([np.cos(ar), np.cos(ac)], 1).astype(np.float32)
        s_ = np.concatenate([np.sin(ar), np.sin(ac)], 1).astype(np.float32)
        n = c.shape[1]
        return (np.ascontiguousarray(c.reshape(16, 128, n).transpose(1, 0, 2)),
                np.ascontiguousarray(s_.reshape(16, 128, n).transpose(1, 0, 2)))
    return tab(64), tab(32)


def _prep_shared(inp):
    f = np.float32
    w_in = inp["w_in"]
    sp = _IN_SPLITS
    dq, dk, dv, gq, gk, gv, mcq, mckv, mkr, gt = np.split(w_in, sp[:-1], axis=-1)
    sh = {}
    sh["w_mod"] = np.ascontiguousarray(inp["w_mod"], f)
    sh["b_mod3"] = np.ascontiguousarray(np.broadcast_to(inp["b_mod"][:, None, :], (2, 3, 6 * D)), f)
    tr = lambda v: np.ascontiguousarray(v.reshape(2, -1, 128).transpose(0, 2, 1), f)
    sh["g_norm1T"] = tr(inp["g_norm1"])
    sh["g_norm2T"] = tr(inp["g_norm2"])
    sh["wA"] = np.ascontiguousarray(np.concatenate([dk, dv, gk, gv, mckv, mkr], -1), f)
    sh["wQ"] = np.ascontiguousarray(np.concatenate([dq, gq, mcq], -1), f)
    sh["wG"] = np.ascontiguousarray(gt, f)
    sh["b_gateT"] = tr(inp["b_gate"])
    lam = np.stack([inp["lam_q1"], inp["lam_k1"], inp["lam_q2"], inp["lam_k2"]], 1)
    sh["lamb"] = np.ascontiguousarray(np.broadcast_to(lam[:, :, None, :], (2, 4, 128, 64)), f)
    sh["g_diff_outT"] = np.ascontiguousarray(inp["g_diff_out"].reshape(2, 128, 1), f)
    bc = lambda v: np.ascontiguousarray(np.broadcast_to(v[:, None, :], (2, 128, v.shape[-1])), f)
    sh["g_gqa_qb"] = bc(inp["g_gqa_q"])
    sh["g_gqa_kb"] = bc(inp["g_gqa_k"])
    sh["g_mla_qb"] = bc(inp["g_mla_q"])
    sh["g_mla_kvb"] = bc(inp["g_mla_kv"])
    sh["w_uq"] = np.ascontiguousarray(inp["w_mla_uq"], f)
    ukv = inp["w_mla_ukv"].reshape(2, 256, 8, 128)
    sh["w_ukv2"] = np.ascontiguousarray(np.concatenate([ukv[..., :64].reshape(2, 256, 512), ukv[..., 64:].reshape(2, 256, 512)], -1), f)
    sh["w_br0"] = np.ascontiguousarray(inp["w_br_diff"], f)
    sh["w_br1"] = np.ascontiguousarray(inp["w_br_gqa"], f)
    sh["w_br2"] = np.ascontiguousarray(inp["w_br_mla"], f)
    sh["w_out"] = np.ascontiguousarray(inp["w_out"], f)
    sh["w_ffn_in"] = np.ascontiguousarray(inp["w_ffn_in"], f)
    sh["w_ffn_out"] = np.ascontiguousarray(inp["w_ffn_out"], f)
    sh["g_finalb"] = np.ascontiguousarray(np.broadcast_to(inp["g_final"][None, :], (128, D)), f)
    sh["ident"] = np.eye(128, dtype=f)
    sel = np.zeros((3, 3, 128), f)
    for k in range(3):
        sel[k, k, :] = 1.0
    sh["sel3"] = sel
    (qc, qs), (mc, ms) = _tables()
    sh["tab_qk_cos"], sh["tab_qk_sin"], sh["tab_m_cos"], sh["tab_m_sin"] = qc, qs, mc, ms
    return sh


def _core_inputs(inp, sh, seqs):
    f = np.float32
    m = dict(sh)
    m["x"] = np.ascontiguousarray(inp["x"][seqs], f)
    m["ctx"] = np.ascontiguousarray(inp["ctx"][seqs], f)
    cs = [inp["c"][s] for s in seqs]
    while len(cs) < 2:
        cs.append(cs[0])
    c3 = np.stack(cs + [inp["c_ctx"]], 0)
    m["cT"] = np.ascontiguousarray(c3.reshape(3, 8, 128).transpose(2, 1, 0), f)
    return m


_CACHE = {}


def kernel(**inputs):
    inp = {k: np.asarray(v) for k, v in inputs.items()}
    ncores = 8
    nseq = 2
    if "prog" not in _CACHE:
        _CACHE["prog"] = build_program(nseq=nseq, depth=2)[0]
    nc = _CACHE["prog"]
    sh = _prep_shared(inp)
    in_maps = [_core_inputs(inp, sh, list(range(c * nseq, (c + 1) * nseq))) for c in range(ncores)]
    res = run_bass_kernel_spmd(nc, in_maps, core_ids=list(range(ncores)))
    out = np.concatenate([np.asarray(r["out"], np.float32) for r in res.results], axis=0)
    return out
```

```python
import math
import numpy as np
import concourse.bass as bass
import concourse.mybir as mybir
from concourse.bass_utils import run_bass_kernel_spmd

F32 = mybir.dt.float32
BF16 = mybir.dt.bfloat16
AF = mybir.ActivationFunctionType
ALU = mybir.AluOpType
AX = mybir.AxisListType

D = 1024
SEQ = 2048
CTX = 256
NKEY = SEQ + CTX
EPS = 1e-6
FFN = 2816
THETA = 10000.0
WA_COLS = 1568
WQ_COLS = 1408


class Res:
    __slots__ = ("name", "w", "r")

    def __init__(self, name):
        self.name = name
        self.w = None
        self.r = {}


class Ins:
    __slots__ = ("eng", "fn", "deps", "sig", "tok", "dma", "dsem")

    def __init__(self, eng, fn, dma=False):
        self.eng = eng
        self.fn = fn
        self.deps = []
        self.sig = False
        self.tok = None
        self.dma = dma
        self.dsem = None


class Prog:
    def __init__(self, nc):
        self.nc = nc
        self.E = {"pe": nc.tensor, "act": nc.scalar, "dve": nc.vector, "pool": nc.gpsimd, "sp": nc.sync}
        self.ins = []
        self.res = {}
        self.dsems = {}
        self.outs = []

    def R(self, name):
        r = self.res.get(name)
        if r is None:
            r = self.res[name] = Res(name)
        return r

    def _track(self, I, reads, writes):
        deps = I.deps

        def add(d, raw):
            if d is None or d is I:
                return
            if (not d.dma) and (not I.dma) and d.eng == I.eng and not raw:
                return
            if (not d.dma) and (not I.dma) and d.eng == I.eng == "pe":
                return
            deps.append(d)

        for r in reads:
            r = self.R(r) if isinstance(r, str) else r
            add(r.w, True)
        for w in writes:
            w = self.R(w) if isinstance(w, str) else w
            add(w.w, False)
            for rd in w.r.values():
                add(rd, False)
        rk = ("d", I.dsem) if I.dma else I.eng
        for r in reads:
            r = self.R(r) if isinstance(r, str) else r
            r.r[rk] = I
        for w in writes:
            w = self.R(w) if isinstance(w, str) else w
            w.w = I
            w.r = {}

    def op(self, eng, fn, reads=(), writes=()):
        I = Ins(eng, fn)
        self._track(I, reads, writes)
        self.ins.append(I)
        return I

    def dma(self, q, fn, semkey, reads=(), writes=(), out=False):
        I = Ins(q, fn, dma=True)
        I.dsem = semkey
        self._track(I, reads, writes)
        self.ins.append(I)
        if out:
            self.outs.append(I)
        return I

    def finalize(self):
        nc = self.nc
        for I in self.ins:
            for d in I.deps:
                d.sig = True
        esem = {e: nc.alloc_semaphore(name="es_" + e) for e in self.E}
        cnt = {e: 0 for e in self.E}
        dcnt = {}
        waited = {e: {} for e in self.E}
        semid = {}

        def wait_on(engname, need):
            eng = self.E[engname]
            wd = waited[engname]
            for key, (sem, val) in need.items():
                if wd.get(key, 0) >= val:
                    continue
                eng.wait_ge(sem, val)
                wd[key] = val

        for I in self.ins:
            need = {}
            for d in I.deps:
                sem, val, key = d.tok
                if key not in need or need[key][1] < val:
                    need[key] = (sem, val)
            wait_on(I.eng, need)
            inst = I.fn(self.E[I.eng])
            if I.dma:
                k = I.dsem
                if k not in self.dsems:
                    self.dsems[k] = nc.alloc_semaphore(name="ds_" + k)
                    dcnt[k] = 0
                dcnt[k] += 16
                inst.then_inc(self.dsems[k], 16)
                I.tok = (self.dsems[k], dcnt[k], "d_" + k)
            elif I.sig:
                cnt[I.eng] += 1
                inst.then_inc(esem[I.eng], 1)
                I.tok = (esem[I.eng], cnt[I.eng], "e_" + I.eng)
        need = {}
        for d in self.outs:
            sem, val, key = d.tok
            if key not in need or need[key][1] < val:
                need[key] = (sem, val)
        wait_on("sp", need)
        need = {}
        for e in ("pe", "act", "dve", "pool"):
            if cnt[e] > 0:
                need["e_" + e] = (esem[e], cnt[e])
        wait_on("sp", need)
        self.counts = dict(cnt)


def fview(t, p0, npart, off, dims):
    ps = t[:].ap[0][0]
    return bass.AP(t, p0 * ps + off, [[ps, npart]] + [[s, c] for (s, c) in dims])


def build_program(nseq=2, depth=2, dbg=None):
    nc = bass.Bass("TRN2", target_bir_lowering=False)
    P = Prog(nc)
    QT = 256
    NS = QT // 128

    def din(name, shape, dt=F32):
        return nc.dram_tensor(name, list(shape), dt, kind="ExternalInput").ap()

    x_d = din("x", [nseq, SEQ, D])
    ctx_d = din("ctx", [nseq, CTX, D])
    cT_d = din("cT", [128, 8, 3])
    wmod_d = din("w_mod", [2, D, 6 * D])
    bmod_d = din("b_mod3", [2, 3, 6 * D])
    gn1_d = din("g_norm1T", [2, 128, 8])
    gn2_d = din("g_norm2T", [2, 128, 8])
    wA_d = din("wA", [2, D, WA_COLS])
    wQ_d = din("wQ", [2, D, WQ_COLS])
    wG_d = din("wG", [2, D, 3 * D])
    bg_d = din("b_gateT", [2, 128, 24])
    lam_d = din("lamb", [2, 4, 128, 64])
    gdo_d = din("g_diff_outT", [2, 128, 1])
    ggq_d = din("g_gqa_qb", [2, 128, 64])
    ggk_d = din("g_gqa_kb", [2, 128, 64])
    gmq_d = din("g_mla_qb", [2, 128, 384])
    gmkv_d = din("g_mla_kvb", [2, 128, 256])
    wuq_d = din("w_uq", [2, 384, 768])
    wukv_d = din("w_ukv2", [2, 256, 1024])
    wbr_d = [din("w_br%d" % b, [2, 512, D]) for b in range(3)]
    wout_d = din("w_out", [2, D, D])
    wfi_d = din("w_ffn_in", [2, D, 2 * FFN])
    wfo_d = din("w_ffn_out", [2, FFN, D])
    gfb_d = din("g_finalb", [128, D])
    ident_d = din("ident", [128, 128])
    sel_d = din("sel3", [3, 3, 128])
    tqc_d = din("tab_qk_cos", [128, 16, 32])
    tqs_d = din("tab_qk_sin", [128, 16, 32])
    tmc_d = din("tab_m_cos", [128, 16, 16])
    tms_d = din("tab_m_sin", [128, 16, 16])
    out_d = nc.dram_tensor("out", [nseq, SEQ, D], F32, kind="ExternalOutput").ap()
    KdT_d = nc.dram_tensor("KdT", [4, 128, NKEY], BF16, kind="Internal").ap()
    KgT_d = nc.dram_tensor("KgT", [128, NKEY], BF16, kind="Internal").ap()
    KmT_d = nc.dram_tensor("KmT", [8, 96, NKEY], BF16, kind="Internal").ap()
    Vd_d = nc.dram_tensor("Vd", [NKEY, 512], BF16, kind="Internal").ap()
    Vg_d = nc.dram_tensor("Vg", [NKEY, 128], BF16, kind="Internal").ap()
    Vm_d = nc.dram_tensor("Vm", [NKEY, 512], BF16, kind="Internal").ap()
    modg_d = nc.dram_tensor("modg_scr", [2, 2, 3, D], F32, kind="Internal").ap()
    WSRC = {"wA": wA_d, "wukv": wukv_d, "wQ": wQ_d, "wuq": wuq_d, "wG": wG_d, "wbr0": wbr_d[0], "wbr1": wbr_d[1],
            "wbr2": wbr_d[2], "wout": wout_d, "wfi": wfi_d, "wfo": wfo_d}
    WT = {
        "wA": [(0, 8, 0, 512), (0, 8, 512, 512), (0, 8, 1024, 512), (0, 8, 1536, 32)],
        "wukv": [(0, 2, 0, 512), (0, 2, 512, 512)],
        "wQ": [(0, 8, 0, 512), (0, 8, 512, 512), (0, 8, 1024, 384)],
        "wuq": [(0, 3, 0, 384), (0, 3, 384, 384)],
        "wG": [(0, 8, c * 512, 512) for c in range(6)],
        "wbr0": [(0, 4, 0, 512), (0, 4, 512, 512)], "wbr1": [(0, 4, 0, 512), (0, 4, 512, 512)],
        "wbr2": [(0, 4, 0, 512), (0, 4, 512, 512)],
        "wout": [(0, 8, 0, 512), (0, 8, 512, 512)],
        "wfi": [(0, 8, t * 512, 512 if t < 5 else 256) for t in range(6)] + [(0, 8, FFN + t * 512, 512 if t < 5 else 256) for t in range(6)],
        "wfo": [(k0, nkc, ct * 512, 512) for ct in range(2) for (k0, nkc) in ((0, 8), (8, 8), (16, 6))],
    }
    WOFF = {}
    WSCR = {}
    cast_jobs = {0: [], 1: []}
    cast_res = {}
    for nm in ("wA", "wukv", "wQ", "wuq", "wG", "wbr0", "wbr1", "wbr2", "wout", "wfi", "wfo"):
        off = 0
        for tl in WT[nm]:
            WOFF[(nm,) + tl] = off
            off += 128 * tl[1] * tl[3]
        WSCR[nm] = nc.dram_tensor(nm + "_b", [2, off], BF16, kind="Internal").ap()
    for l_ in range(2):
        for nm in ("wA", "wukv", "wQ", "wuq", "wG", "wbr0", "wbr1", "wbr2", "wout", "wfi", "wfo"):
            src_ = WSRC[nm]
            scr = WSCR[nm]
            tot = scr.shape[1]
            for tl in WT[nm]:
                k0, kc, c0, ncols = tl
                keys = []
                base = l_ * tot + WOFF[(nm,) + tl]
                for kk in range(kc):
                    if nm == "wbr1":
                        parts = [(kv_ * 64, 64, kv_ * 256 + (k0 + kk) * 64) for kv_ in range(2)]
                    else:
                        parts = [(0, 128, (k0 + kk) * 128)]
                    for (p0, npp, r0) in parts:
                        k_ = "cw_%s_%d_%d_%d_%d_%d" % (nm, l_, k0, c0, kk, p0)
                        keys.append(k_)
                        dst_ = bass.AP(scr.tensor, base + p0 * kc * ncols + kk * ncols, [[kc * ncols, npp], [1, ncols]])
                        cast_jobs[l_].append((dst_, src_[l_, r0:r0 + npp, c0:c0 + ncols], "cs_%s_%d" % (nm, l_), k_))
                cast_res.setdefault((nm, l_), []).extend(keys)

    def drain_casts(l_, n):
        for _ in range(min(n, len(cast_jobs[l_]))):
            dst_, src_, sem_, k_ = cast_jobs[l_].pop(0)
            P.dma("pool", (lambda e, dst_=dst_, src_=src_: e.dma_start(out=dst_, in_=src_)), sem_, (), [k_])

    dbg_d = None
    if dbg is not None:
        dbg_d = nc.dram_tensor("dbg", [128, dbg], F32, kind="ExternalOutput").ap()

    def sb(name, shape, dt):
        return nc.alloc_sbuf_tensor("s_" + name, shape, dt)
    x_sb = sb("x_sb", [128, 16, D], F32)
    xc_sb = sb("xc_sb", [128, 2, D], F32)
    ident_f = sb("ident_f", [128, 128], F32)
    ident_b = sb("ident_b", [128, 128], BF16)
    ones_b = sb("ones_b", [128, 128], BF16)
    tqc = sb("tqc", [128, 16, 32], F32)
    tqs = sb("tqs", [128, 16, 32], F32)
    tmc = sb("tmc", [128, 16, 16], F32)
    tms = sb("tms", [128, 16, 16], F32)
    cT = sb("cT", [128, 8, 3], F32)
    modT = sb("modT", [128, 2, 48, 3], F32)
    gsv = sb("gsv", [128, 2, 2, 8, 3], F32)
    gn = sb("gn", [128, 2, 2, 8], F32)
    bgT = sb("bgT", [128, 2, 24], F32)
    lamb = sb("lamb", [128, 4, 64], F32)
    lamv = sb("lamv", [128, 2, 4], F32)
    gdo = sb("gdo", [128, 2, 1], F32)
    ggq = sb("ggq", [128, 64], F32)
    ggk = sb("ggk", [128, 64], F32)
    gmq = sb("gmq", [128, 384], F32)
    gmkv = sb("gmkv", [128, 256], F32)
    g1b = sb("g1b", [128, D], F32)
    g2b = sb("g2b", [128, D], F32)
    NW = 4
    wring = [sb("wring%d" % i, [128, 8, 512], BF16) for i in range(NW)]
    hT = sb("hT", [128, 8, 256], BF16)
    junk = sb("junk", [128, D], BF16)
    st = sb("st", [128, 16], F32)
    diag = sb("diag", [128, 128], F32)
    zt = [sb("zt%d" % i, [128, 512], F32) for i in range(4)]
    qtm = sb("qtm", [128, 768], BF16)
    vtm = sb("vtm", [128, 512], BF16)
    kstage = sb("kstage", [128, 8, 128], BF16)
    QdT = sb("QdT", [128, 8, QT], BF16)
    QgT = sb("QgT", [128, 8, QT], BF16)
    QmT = sb("QmT", [128, 8, QT], BF16)
    cqnT = sb("cqnT", [128, 3, QT], BF16)
    kvh = sb("kvh", [128, 4 * NKEY], BF16)
    kbuf = [kvh[:, i * NKEY:(i + 1) * NKEY] for i in range(2)]
    vbuf = [kvh[:, (2 + i) * NKEY:(3 + i) * NKEY].rearrange("p (a b) -> p a b", a=18) for i in range(2)]
    hid = kvh[:, 0:22 * QT].rearrange("p (a b) -> p a b", a=22)
    HIDK = ["kbuf0", "kbuf1", "vbuf0"]
    PT = [sb("PT%d" % i, [128, 512], BF16) for i in range(4)]
    OT = [sb("OT%d" % b, [128, 4, QT], BF16) for b in range(3)]
    of32 = sb("of32", [128, QT], F32)
    sqb = sb("sqb", [128, QT], BF16)
    ysum = sb("ysum", [128, 8, QT], F32)
    yT = sb("yT", [128, 8, QT], BF16)
    outst = [ysum[:, 4 * i:4 * i + 4, :].rearrange("p a b -> p (a b)") for i in range(2)]
    sig = [sb("sig%d" % i, [128, QT], F32) for i in range(2)]
    pmm = [nc.alloc_psum_tensor("pmm%d" % i, [128, 512], F32) for i in range(3)]
    psc = [nc.alloc_psum_tensor("psc%d" % i, [128, 512], F32) for i in range(3)]
    pacc = [nc.alloc_psum_tensor("pacc%d" % i, [128, 512], F32) for i in range(2)]

    rot = {"mm": 0, "sc": 0, "w": 0, "pt": 0, "k": 0, "v": 0, "os": 0, "wm": 0, "zt": 0}

    def nxt(key, n):
        i = rot[key]
        rot[key] = (i + 1) % n
        return i

    def mm_bank():
        i = nxt("mm", 3)
        return pmm[i], "pmm%d" % i

    def dma_sp(out, in_, key, reads=(), writes=(), is_out=False):
        return P.dma("sp", lambda e: e.dma_start(out=out, in_=in_, allow_slow_non_contiguous=True), key, reads, writes, out=is_out)

    def dma_pool(out, in_, key, reads=(), writes=()):
        return P.dma("pool", lambda e: e.dma_start(out=out, in_=in_, allow_slow_non_contiguous=True), key, reads, writes)

    def act(out, in_, func, reads, writes, **kw):
        return P.op("act", lambda e: e.activation(out=out, in_=in_, func=func, **kw), reads, writes)

    def tt(out, in0, in1, op, reads, writes):
        return P.op("dve", lambda e: e.tensor_tensor(out=out, in0=in0, in1=in1, op=op), reads, writes)

    def ts(out, in0, s1, s2, op0, op1, reads, writes):
        if op1 is None:
            return P.op("dve", lambda e: e.tensor_scalar(out=out, in0=in0, scalar1=s1, scalar2=None, op0=op0), reads, writes)
        return P.op("dve", lambda e: e.tensor_scalar(out=out, in0=in0, scalar1=s1, scalar2=s2, op0=op0, op1=op1), reads, writes)

    def stt(out, in0, scalar, in1, op0, op1, reads, writes):
        return P.op("dve", lambda e: e.scalar_tensor_tensor(out=out, in0=in0, scalar=scalar, in1=in1, op0=op0, op1=op1), reads, writes)

    def cp(out, in_, reads, writes):
        return P.op("dve", lambda e: e.tensor_copy(out=out, in_=in_), reads, writes)

    def mm(out, lhsT, rhs, start, stop, reads, writes):
        return P.op("pe", lambda e: e.matmul(out, lhsT=lhsT, rhs=rhs, start=start, stop=stop), reads, writes)

    def tp(out, in_, reads, writes, f32=False):
        idn = ident_f if f32 else ident_b
        np_ = in_.shape[0]
        return P.op("pe", lambda e: e.transpose(out, in_, idn[0:np_, 0:np_]), list(reads) + ["ident"], writes)

    def wtile(nm, l_, k0, kc, c0, ncols):
        i = nxt("w", NW)
        t = wring[i]
        scr = WSCR[nm]
        base = l_ * scr.shape[1] + WOFF[(nm, k0, kc, c0, ncols)]
        src = bass.AP(scr.tensor, base, [[kc * ncols, 128], [ncols, kc], [1, ncols]])
        dma_pool(t[:, 0:kc, 0:ncols], src, "w%d" % i, cast_res[(nm, l_)], ["w%d" % i])
        return t, "w%d" % i

    def rsqrt_small(ap, n, key):
        act(ap, ap, AF.Ln, [key], [key])
        act(ap, ap, AF.Exp, [key], [key], scale=-0.5)

    dma_sp(ident_f[:], ident_d, "c0", (), ["ident"])
    dma_pool(ident_b[:], ident_d, "c1", (), ["ident"])
    P.op("dve", lambda e: e.memset(ones_b[:], 1.0), (), ["ones"])
    P.op("dve", lambda e: e.memset(QdT[:], 0.0), (), ["QdT"])
    P.op("dve", lambda e: e.memset(QgT[:], 0.0), (), ["QgT"])
    P.op("dve", lambda e: e.memset(QmT[:], 0.0), (), ["QmT"])
    dma_sp(tqc[:], tqc_d, "c3", (), ["tab"])
    dma_sp(tqs[:], tqs_d, "c4", (), ["tab"])
    dma_sp(tmc[:], tmc_d, "c5", (), ["tab"])
    dma_sp(tms[:], tms_d, "c6", (), ["tab"])
    dma_sp(cT[:], cT_d, "c8", (), ["cT"])
    dma_sp(gn[:, :, 0, :], gn1_d.rearrange("l p c -> p l c"), "c9", (), ["gn"])
    dma_sp(gn[:, :, 1, :], gn2_d.rearrange("l p c -> p l c"), "c10", (), ["gn"])
    dma_sp(bgT[:], bg_d.rearrange("l p c -> p l c"), "c11", (), ["bgT"])
    dma_sp(gdo[:], gdo_d.rearrange("l p c -> p l c"), "c12", (), ["gdo"])
    act(cT[:], cT[:], AF.Silu, ["cT"], ["cT"])

    lam_init = [0.8 - 0.6 * math.exp(-0.3 * l) for l in range(2)]
    for l in range(depth):
        dma_sp(lamb[:], lam_d[l].rearrange("k p c -> p k c"), "c17", (), ["lamb"])
        tt(zt[0][:, 0:64], lamb[:, 0, :], lamb[:, 1, :], ALU.mult, ["lamb"], ["zt0"])
        tt(zt[0][:, 64:128], lamb[:, 2, :], lamb[:, 3, :], ALU.mult, ["lamb"], ["zt0"])
        P.op("dve", lambda e: e.tensor_reduce(out=st[:, 0:2], in_=zt[0][:, 0:128].rearrange("p (a b) -> p a b", a=2), axis=AX.X, op=ALU.add), ["zt0"], ["st"])
        act(st[:, 0:2], st[:, 0:2], AF.Exp, ["st"], ["st"])
        tt(st[:, 2:3], st[:, 1:2], st[:, 0:1], ALU.subtract, ["st"], ["st"])
        _l = l
        ts(lamv[:, l, 0:1], st[:, 2:3], -lam_init[l], None, ALU.add, None, ["st"], ["lamv"])
        ts(gdo[:, l, :], gdo[:, l, :], 1.0 - lam_init[l], None, ALU.mult, None, ["gdo"], ["gdo"])
        for cc in range(24):
            i = nxt("w", NW)
            wt = wring[i][:].rearrange("p a b -> p (a b)").bitcast(F32).rearrange("p (a b) -> p a b", a=8)
            wmk = "w%d" % i
            dma_sp(wt, wmod_d[l, :, cc * 256:(cc + 1) * 256].rearrange("(kc p) n -> p kc n", p=128), wmk, (), [wmk])
            bmt = zt[1]
            dma_sp(bmt[0:3, 0:256], bmod_d[l, :, cc * 256:(cc + 1) * 256], "bm", (), ["zt1"])
            ps, pk = mm_bank()
            for kc in range(8):
                mm(ps[0:3, 0:256], cT[:, kc, :], wt[:, kc, :], kc == 0, kc == 7, ["cT", wmk], [pk])
            tt(bmt[0:3, 0:256], ps[0:3, 0:256], bmt[0:3, 0:256], ALU.add, [pk, "zt1"], ["zt1"])
            which = cc // 4
            if which in (2, 5):
                gi = 0 if which == 2 else 1
                dma_sp(modg_d[l, gi, :, (cc % 4) * 256:(cc % 4) * 256 + 256], bmt[0:3, 0:256], "mg", ["zt1"], ["modg%d" % l])
            else:
                ps2, pk2 = mm_bank()
                for q in range(2):
                    mm(ps2[:, q * 3:(q + 1) * 3], bmt[0:3, q * 128:(q + 1) * 128], ident_f[0:3, 0:3], True, True, ["zt1", "ident"], [pk2])
                cp(modT[:, l, cc * 2:(cc + 1) * 2, :], ps2[:, 0:6].rearrange("p (a b) -> p a b", a=2), [pk2], ["modT"])
        for ni, wh in ((0, 1), (1, 4)):
            for s in range(3):
                stt(gsv[:, l, ni, :, s], modT[:, l, wh * 8:(wh + 1) * 8, s], 1.0, gn[:, l, ni, :], ALU.add, ALU.mult, ["modT", "gn"], ["gsv"])

    def bcast_gates(l, s):
        for gi, gt, gk in ((0, g1b, "g1b"), (1, g2b, "g2b")):
            src = modg_d[l, gi, s:s + 1, :]
            srcb = bass.AP(src.tensor, src.offset, [[0, 128], [1, D]])
            dma_sp(gt[:], srcb, "gb%d" % gi, ["modg%d" % l], [gk])

    def rms_hT(xap, xres, l, ni, s, col0):
        act(junk[:], xap, AF.Square, [xres], ["junk", "st4"], accum_out=st[:, 4:5])
        ts(st[:, 5:6], st[:, 4:5], 1.0 / D, EPS, ALU.mult, ALU.add, ["st4"], ["st5"])
        rsqrt_small(st[:, 5:6], 1, "st5")
        ts(diag[:], ident_f[:], st[:, 5:6], None, ALU.mult, None, ["ident", "st5"], ["diag"])
        banks = [mm_bank(), mm_bank()]
        for c in range(8):
            ps, pk = banks[c // 4]
            mm(ps[:, (c % 4) * 128:(c % 4 + 1) * 128], xap[:, c * 128:(c + 1) * 128], diag[:], True, True, [xres, "diag"], [pk])
        wh = 0 if ni == 0 else 3
        for c in range(8):
            ps, pk = banks[c // 4]
            act(hT[:, c, col0:col0 + 128], ps[:, (c % 4) * 128:(c % 4 + 1) * 128], AF.Identity, [pk, "gsv", "modT"], ["hT"],
                scale=gsv[:, l, ni, c, s:s + 1], bias=modT[:, l, wh * 8 + c, s:s + 1])

    def group_norm(src, ncol, G, gvec, dst_f32, key):
        gs = ncol // G
        sq = zt[3]
        act(sq[:, 0:ncol], src, AF.Square, [key], ["zt3"])
        P.op("dve", lambda e: e.tensor_reduce(out=st[:, 8:8 + G], in_=sq[:, 0:ncol].rearrange("p (g d) -> p g d", g=G), axis=AX.X, op=ALU.add), ["zt3"], ["st8"])
        ts(st[:, 8:8 + G], st[:, 8:8 + G], 1.0 / gs, EPS, ALU.mult, ALU.add, ["st8"], ["st8"])
        rsqrt_small(st[:, 8:8 + G], G, "st8")
        srcv = src.rearrange("p (g d) -> p g d", g=G)
        dstv = dst_f32[:, 0:ncol].rearrange("p (g d) -> p g d", g=G)
        rb = fview(st, 0, 128, 8, [(1, G), (0, gs)])
        tt(dstv, srcv, rb, ALU.mult, [key, "st8"], ["_gn_dst"])
        gb = bass.AP(gvec.tensor, gvec.offset, [list(gvec.ap[0]), [0, G], [1, gs]])
        tt(dstv, dstv, gb, ALU.mult, ["_gn_dst", "gvec"], ["_gn_dst"])

    def rope(src, srckey, H, hd, dst_bf, dstkey, cosap, sinap, nfreq, dst_hstride=None, dst_off=0, src_hstride=None, src_off=0):
        sh = src_hstride or hd
        dh = dst_hstride or hd

        def sv(t, off, hs, half):
            return fview(t, 0, 128, off + half * nfreq, [(hs, H), (2 * nfreq, 2), (1, nfreq)])

        cb = bass.AP(cosap.tensor, cosap.offset, [list(cosap.ap[0]), [0, H], [nfreq, 2], [1, nfreq]])
        sbp = bass.AP(sinap.tensor, sinap.offset, [list(sinap.ap[0]), [0, H], [nfreq, 2], [1, nfreq]])
        n = H * 2 * nfreq
        ta = zt[1]
        tb = zt[2]
        tav = fview(ta, 0, 128, 0, [(2 * nfreq, H), (nfreq, 2), (1, nfreq)])
        tbv = fview(tb, 0, 128, 0, [(2 * nfreq, H), (nfreq, 2), (1, nfreq)])
        x1 = sv(src, src_off, sh, 0)
        x2 = sv(src, src_off, sh, 1)
        tt(tav, x1, cb, ALU.mult, [srckey, "tab"], ["zt1"])
        tt(tbv, x2, sbp, ALU.mult, [srckey, "tab"], ["zt2"])
        tt(sv(dst_bf, dst_off, dh, 0), tav, tbv, ALU.subtract, ["zt1", "zt2"], [dstkey])
        tt(tav, x1, sbp, ALU.mult, [srckey, "tab"], ["zt1"])
        tt(tbv, x2, cb, ALU.mult, [srckey, "tab"], ["zt2"])
        tt(sv(dst_bf, dst_off, dh, 1), tav, tbv, ALU.add, ["zt1", "zt2"], [dstkey])

    def tp_bank():
        ps, pk = mm_bank()
        return ps[:].bitcast(BF16), pk

    def transposes_to(src_bf, srckey, blocks, dst_fn, dstkey):
        for i0 in range(0, len(blocks), 8):
            grp = blocks[i0:i0 + 8]
            ptp, ptk = tp_bank()
            for j, (c0, ncol) in enumerate(grp):
                tp(ptp[0:ncol, j * 128:(j + 1) * 128], src_bf[:, c0:c0 + ncol], [srckey], [ptk])
            ncol = grp[0][1]
            dst = dst_fn(i0, len(grp))
            cp(dst, ptp[0:ncol, 0:len(grp) * 128].rearrange("p (a b) -> p a b", a=len(grp)), [ptk], [dstkey])

    def transposes_pad(src_bf, srckey, Qp, qk, i):
        ptp, ptk = tp_bank()
        for j in range(4):
            tp(ptp[:, j * 128:(j + 1) * 128], src_bf[:, j * 128:(j + 1) * 128], [srckey], [ptk])
        QTn = Qp.shape[2]
        cp(fview(Qp, 0, 64, i * 128, [(2 * QTn, 4), (1, 128)]), ptp[0:64, 0:512].rearrange("p (a b) -> p a b", a=4), [ptk], [qk])
        cp(fview(Qp, 64, 64, QTn + i * 128, [(2 * QTn, 4), (1, 128)]), ptp[64:128, 0:512].rearrange("p (a b) -> p a b", a=4), [ptk], [qk])

    ckvnTa = sb("ckvnTa", [128, 2, 256], BF16)
    krtma = sb("krtma", [128, 2, 32], BF16)
    kmtm = sb("kmtm", [128, 768], BF16)
    kst2 = [kstage, sb("kstage1", [128, 8, 128], BF16)]
    vtm2 = [vtm, sb("vtm1", [128, 512], BF16)]

    def kst():
        i = nxt("k", 2)
        return kst2[i], "kstage%d" % i, "kst%d" % i

    def vst():
        i = nxt("v", 2)
        return vtm2[i], "vtm%d" % i, "vst%d" % i

    def phase_a_tile(l, xbuf, xkey, sub0, nsub, key0, latent, lat_sub0, mslot):
        for i in range(nsub):
            rms_hT(xbuf[:, sub0 + i, :], "%s%d" % (xkey, sub0 + i), l, 0, mslot, i * 128)
        groups = [(0, 512), (512, 512), (1024, 512), (1536, 32)]
        for gi, (c0, ncol) in enumerate(groups):
            wt, wk = wtile("wA", l, 0, 8, c0, ncol)
            for i in range(nsub):
                ks = key0 + i * 128
                kt = ks // 128
                ps, pk = mm_bank()
                for kc in range(8):
                    mm(ps[:, 0:ncol], hT[:, kc, i * 128:(i + 1) * 128], wt[:, kc, 0:ncol], kc == 0, kc == 7, ["hT", wk], [pk])
                tsub = lat_sub0 + i
                if gi == 0:
                    if latent:
                        rope(ps, pk, 8, 64, qtm, "qtm", tqc[:, tsub, :], tqs[:, tsub, :], 16)
                    else:
                        cp(qtm[:, 0:512], ps[:], [pk], ["qtm"])
                    kb_, kbk, ksem = kst()
                    transposes_to(qtm, "qtm", [(j * 128, 128) for j in range(4)], lambda i0, n: kb_[:, 0:4, :], kbk)
                    dma_sp(KdT_d[:, :, ks:ks + 128].rearrange("j p n -> p j n"), kb_[:, 0:4, :], ksem, [kbk], ["KdT_%d" % kt])
                elif gi == 1:
                    vb_, vbk, vsem = vst()
                    cp(vb_[:], ps[:], [pk], [vbk])
                    dma_sp(Vd_d[ks:ks + 128, :], vb_[:], vsem, [vbk], ["Vd_%d" % kt])
                elif gi == 2:
                    group_norm(ps[:, 0:128], 128, 2, ggk[:], zt[0], pk)
                    if latent:
                        rope(zt[0], "_gn_dst", 2, 64, qtm, "qtm", tqc[:, tsub, :], tqs[:, tsub, :], 16)
                    else:
                        cp(qtm[:, 0:128], zt[0][:, 0:128], ["_gn_dst"], ["qtm"])
                    kb_, kbk, ksem = kst()
                    transposes_to(qtm, "qtm", [(0, 128)], lambda i0, n: kb_[:, 0:1, :], kbk)
                    dma_sp(KgT_d[:, ks:ks + 128], kb_[:, 0, :], ksem, [kbk], ["KgT_%d" % kt])
                    vb_, vbk, vsem = vst()
                    cp(vb_[:, 0:128], ps[:, 128:256], [pk], [vbk])
                    dma_sp(Vg_d[ks:ks + 128, :], vb_[:, 0:128], vsem, [vbk], ["Vg_%d" % kt])
                    group_norm(ps[:, 256:512], 256, 1, gmkv[:], zt[0], pk)
                    cp(qtm[:, 0:256], zt[0][:, 0:256], ["_gn_dst"], ["qtm"])
                    transposes_to(qtm, "qtm", [(0, 128), (128, 128)], lambda i0, n: ckvnTa[:, :, i * 128:(i + 1) * 128], "ckvnTa")
                else:
                    if latent:
                        rope(ps, pk, 1, 32, krtma, "krtma", tmc[:, tsub, :], tms[:, tsub, :], 8, dst_off=i * 32)
                    else:
                        cp(krtma[:, i, :], ps[:, 0:32], [pk], ["krtma"])
        wkt, wkk = wtile("wukv", l, 0, 2, 0, 512)
        wvt, wvk = wtile("wukv", l, 0, 2, 512, 512)
        for i in range(nsub):
            ks = key0 + i * 128
            kt = ks // 128
            ps, pk = mm_bank()
            for kc in range(2):
                mm(ps[:], ckvnTa[:, kc, i * 128:(i + 1) * 128], wkt[:, kc, :], kc == 0, kc == 1, ["ckvnTa", wkk], [pk])
            cp(fview(kmtm, 0, 128, 0, [(96, 8), (1, 64)]), ps[:].rearrange("p (h d) -> p h d", h=8), [pk], ["kmtm"])
            cp(fview(kmtm, 0, 128, 64, [(96, 8), (1, 32)]), fview(krtma, 0, 128, i * 32, [(0, 8), (1, 32)]), ["krtma"], ["kmtm"])
            kb_, kbk, ksem = kst()
            transposes_to(kmtm, "kmtm", [(h * 96, 96) for h in range(8)], lambda i0, n: kb_[0:96, :, :], kbk)
            dma_sp(KmT_d[:, :, ks:ks + 128].rearrange("h p n -> p h n"), kb_[0:96, :, :], ksem, [kbk], ["KmT_%d" % kt])
            ps2, pk2 = mm_bank()
            for kc in range(2):
                mm(ps2[:], ckvnTa[:, kc, i * 128:(i + 1) * 128], wvt[:, kc, :], kc == 0, kc == 1, ["ckvnTa", wvk], [pk2])
            vb_, vbk, vsem = vst()
            cp(vb_[:], ps2[:], [pk2], [vbk])
            dma_sp(Vm_d[ks:ks + 128, :], vb_[:], vsem, [vbk], ["Vm_%d" % kt])

    def phase_b_tile(l, xbuf, xkey, sub0, nsub, latent, lat_sub0, mslot, nk):
        QN = nsub * 128
        xk = ["%s%d" % (xkey, sub0 + i) for i in range(nsub)]
        for i in range(nsub):
            rms_hT(xbuf[:, sub0 + i, :], xk[i], l, 0, mslot, i * 128)
        for gi, (c0, ncol) in enumerate([(0, 512), (512, 512), (1024, 384)]):
            wt, wk = wtile("wQ", l, 0, 8, c0, ncol)
            for i in range(nsub):
                ps, pk = mm_bank()
                for kc in range(8):
                    mm(ps[:, 0:ncol], hT[:, kc, i * 128:(i + 1) * 128], wt[:, kc, 0:ncol], kc == 0, kc == 7, ["hT", wk], [pk])
                tsub = lat_sub0 + i
                if gi == 0:
                    if latent:
                        rope(ps, pk, 8, 64, qtm, "qtm", tqc[:, tsub, :], tqs[:, tsub, :], 16)
                    else:
                        cp(qtm[:, 0:512], ps[:], [pk], ["qtm"])
                    transposes_pad(qtm, "qtm", QdT, "QdT", i)
                elif gi == 1:
                    group_norm(ps[:, 0:512], 512, 8, ggq[:], zt[0], pk)
                    for kv in range(2):
                        if latent:
                            rope(zt[0], "_gn_dst", 4, 64, qtm, "qtm", tqc[:, tsub, :], tqs[:, tsub, :], 16,
                                 dst_hstride=128, dst_off=kv * 64, src_hstride=64, src_off=kv * 256)
                        else:
                            cp(fview(qtm, 0, 128, kv * 64, [(128, 4), (1, 64)]), fview(zt[0], 0, 128, kv * 256, [(64, 4), (1, 64)]), ["_gn_dst"], ["qtm"])
                    transposes_pad(qtm, "qtm", QgT, "QgT", i)
                else:
                    group_norm(ps[:, 0:384], 384, 1, gmq[:], zt[0], pk)
                    cp(qtm[:, 0:384], zt[0][:, 0:384], ["_gn_dst"], ["qtm"])
                    transposes_to(qtm, "qtm", [(j * 128, 128) for j in range(3)], lambda i0, n: cqnT[:, 0:3, i * 128:(i + 1) * 128], "cqnT")
        for hh in range(2):
            wt, wk = wtile("wuq", l, 0, 3, hh * 384, 384)
            for i in range(nsub):
                tsub = lat_sub0 + i
                ps, pk = mm_bank()
                for kc in range(3):
                    mm(ps[:, 0:384], cqnT[:, kc, i * 128:(i + 1) * 128], wt[:, kc, 0:384], kc == 0, kc == 2, ["cqnT", wk], [pk])
                cp(fview(qtm, 0, 128, 0, [(96, 4), (1, 64)]), fview(ps, 0, 128, 0, [(96, 4), (1, 64)]), [pk], ["qtm"])
                if latent:
                    rope(ps, pk, 4, 32, qtm, "qtm", tmc[:, tsub, :], tms[:, tsub, :], 8, dst_hstride=96, dst_off=64, src_hstride=96, src_off=64)
                else:
                    cp(fview(qtm, 0, 128, 64, [(96, 4), (1, 32)]), fview(ps, 0, 128, 64, [(96, 4), (1, 32)]), [pk], ["qtm"])
                transposes_to(qtm, "qtm", [(h * 96, 96) for h in range(4)], lambda i0, n: QmT[0:96, hh * 4:(hh + 1) * 4, i * 128:(i + 1) * 128], "QmT")

        nkt = nk

        def kv_reads(name):
            return ["%s_%d" % (name, t) for t in range(nkt)]

        def load_k(src, npart, rd):
            i = nxt("k2", 2)
            kb_ = kbuf[i]
            dma_sp(kb_[0:npart, 0:nk * 128], src, "kb%d" % i, kv_reads(rd), ["kbuf%d" % i])
            return kb_, "kbuf%d" % i

        def load_v(src, rd):
            i = nxt("v2", 2)
            vb_ = vbuf[i]
            dma_sp(vb_[:, 0:nk, :], src.rearrange("(kc p) d -> p kc d", p=128), "vb%d" % i, kv_reads(rd), ["vbuf%d" % i])
            return vb_, "vbuf%d" % i

        sc_d = 64 ** -0.5
        sc_m = 96 ** -0.5
        steps = []

        def add_head(kget, vget, qap, qkey, co, scale, fin):
            for kc0 in range(0, nk, 2):
                n2 = min(2, nk - kc0)

                def score(kc0=kc0, n2=n2):
                    kb_, kbk = kget()
                    vget()
                    si = nxt("sc", 3)
                    ps = psc[si]
                    pk = "psc%d" % si
                    for u in range(n2):
                        kc = kc0 + u
                        mm(ps[:, u * QN:(u + 1) * QN], kb_[:, kc * 128:(kc + 1) * 128], qap, True, True, [kbk, qkey], [pk])
                    pi = nxt("pt", 4)
                    pt = PT[pi]
                    ptk = "PT%d" % pi
                    act(pt[:, 0:n2 * QN], ps[:, 0:n2 * QN], AF.Exp, [pk], [ptk], scale=scale)
                    return pt, ptk

                def pv(st_, kc0=kc0, n2=n2):
                    pt, ptk = st_
                    vb_, vbk = vget()
                    for u in range(n2):
                        kc = kc0 + u
                        mm(pacc[0][:, co:co + QN], vb_[:, kc, :], pt[:, u * QN:(u + 1) * QN], kc == 0, kc == nk - 1, [vbk, ptk], ["pacc0"])
                        mm(pacc[1][:, co:co + QN], ones_b[:, 0:128], pt[:, u * QN:(u + 1) * QN], kc == 0, kc == nk - 1, ["ones", ptk], ["pacc1"])
                steps.append((score, pv, fin if kc0 + 2 >= nk else None))

        def lazy(fn):
            box = []

            def get():
                if not box:
                    box.append(fn())
                return box[0]
            return get

        def cpa(out, in_, reads, writes):
            return P.op("act", lambda e: e.activation(out=out, in_=in_, func=AF.Copy), reads, writes)

        def fin_pair(OTt, otk, jj):
            def s0():
                cpa(zt[1][0:64, 0:QN], pacc[1][0:64, 0:QN], ["pacc1"], ["zt1"])
                cpa(zt[1][64:128, 0:QN], pacc[1][64:128, QN:2 * QN], ["pacc1"], ["zt1"])
                cpa(zt[0][0:64, 0:QN], pacc[0][0:64, 0:QN], ["pacc0"], ["zt0"])
                cpa(zt[0][64:128, 0:QN], pacc[0][64:128, QN:2 * QN], ["pacc0"], ["zt0"])
                P.op("dve", lambda e: e.reciprocal(out=zt[1][:, 0:QN], in_=zt[1][:, 0:QN]), ["zt1"], ["zt1"])
                tt(OTt[:, jj, 0:QN], zt[0][:, 0:QN], zt[1][:, 0:QN], ALU.mult, ["zt0", "zt1"], [otk])
            return [s0]

        def fin_diff(h):
            box = {}

            def s0():
                cpa(zt[1][:, 0:2 * QN], pacc[1][:, 0:2 * QN], ["pacc1"], ["zt1"])
                cpa(zt[0][:, 0:2 * QN], pacc[0][:, 0:2 * QN], ["pacc0"], ["zt0"])
                P.op("dve", lambda e: e.reciprocal(out=zt[1][:, 0:2 * QN], in_=zt[1][:, 0:2 * QN]), ["zt1"], ["zt1"])
                tt(zt[0][:, 0:2 * QN], zt[0][:, 0:2 * QN], zt[1][:, 0:2 * QN], ALU.mult, ["zt0", "zt1"], ["zt0"])
                stt(of32[:, 0:QN], zt[0][:, QN:2 * QN], lamv[:, l, 0:1], zt[0][:, 0:QN], ALU.mult, ALU.add, ["zt0", "lamv"], ["of32"])
                tt(sqb[:, 0:QN], of32[:, 0:QN], of32[:, 0:QN], ALU.mult, ["of32"], ["sqb"])

            def s1():
                ps, pk = mm_bank()
                box["ps"] = (ps, pk)
                mm(ps[:, 0:QN], ones_b[:, 0:128], sqb[:, 0:QN], True, True, ["ones", "sqb"], [pk])
                ts(zt[2][:, 0:QN], ps[:, 0:QN], 1.0 / 128, EPS, ALU.mult, ALU.add, [pk], ["zt2"])

            def s2():
                rsqrt_small(zt[2][:, 0:QN], QN, "zt2")

            def s3():
                stt(OT[0][:, h, 0:QN], of32[:, 0:QN], gdo[:, l, 0:1], zt[2][:, 0:QN], ALU.mult, ALU.mult, ["of32", "gdo", "zt2"], ["OT0"])
            return [s0, s1, s2, s3]

        for h in range(4):
            kget = lazy(lambda h=h: load_k(KdT_d[h, :, 0:nk * 128], 128, "KdT"))
            vget = lazy(lambda h=h: load_v(Vd_d[0:nk * 128, h * 128:(h + 1) * 128], "Vd"))
            for m in range(2):
                add_head(kget, vget, QdT[:, h * 2 + m, 0:QN], "QdT", m * QN, sc_d, fin_diff(h) if m == 1 else None)
        kget_g = lazy(lambda: load_k(KgT_d[:, 0:nk * 128], 128, "KgT"))
        vget_g = lazy(lambda: load_v(Vg_d[0:nk * 128, :], "Vg"))
        for j in range(4):
            for kv in range(2):
                add_head(kget_g, vget_g, QgT[:, j * 2 + kv, 0:QN], "QgT", kv * QN, sc_d, fin_pair(OT[1], "OT1", j) if kv == 1 else None)
        for jj in range(4):
            vget = lazy(lambda jj=jj: load_v(Vm_d[0:nk * 128, jj * 128:(jj + 1) * 128], "Vm"))
            for h in (2 * jj, 2 * jj + 1):
                kget = lazy(lambda h=h: load_k(KmT_d[h, :, 0:nk * 128], 96, "KmT"))
                add_head(kget, vget, QmT[:, h, 0:QN], "QmT", (h % 2) * QN, sc_m, fin_pair(OT[2], "OT2", jj) if h % 2 == 1 else None)
        DEPTH_P = 2
        FSTEP = 3 if nk >= 12 else 0
        pend = []
        deferred = []
        tick = [0]

        def run_due(all_=False):
            keep = []
            for (due, fn_) in deferred:
                if all_ or due <= tick[0]:
                    fn_()
                else:
                    keep.append((due, fn_))
            deferred[:] = keep

        def retire(item):
            pv0, fin0, st0 = item
            pv0(st0)
            if fin0 is not None:
                for k_, fn_ in enumerate(fin0):
                    if k_ == 0:
                        fn_()
                    else:
                        deferred.append((tick[0] + k_ * FSTEP, fn_))

        for (score, pv, fin) in steps:
            st_ = score()
            pend.append((pv, fin, st_))
            if len(pend) > DEPTH_P:
                retire(pend.pop(0))
            tick[0] += 1
            run_due()
        while pend:
            retire(pend.pop(0))
            tick[0] += 1
            run_due()
        run_due(all_=True)
        for b in range(3):
            for jj in range(2):
                wg, wgk = wtile("wG", l, 0, 8, b * 1024 + jj * 512, 512)
                wb_, wbk = wtile("wbr%d" % b, l, 0, 4, jj * 512, 512)
                for j4 in range(4):
                    j = jj * 4 + j4
                    psg, pkg = mm_bank()
                    for kc in range(8):
                        mm(psg[:, 0:QN], wg[:, kc, j4 * 128:(j4 + 1) * 128], hT[:, kc, 0:QN], kc == 0, kc == 7, [wgk, "hT"], [pkg])
                    psb, pkb = mm_bank()
                    for kc in range(4):
                        mm(psb[:, 0:QN], wb_[:, kc, j4 * 128:(j4 + 1) * 128], OT[b][:, kc, 0:QN], kc == 0, kc == 3, [wbk, "OT%d" % b], [pkb])
                    si = nxt("sg", 2)
                    sg = sig[si]
                    sgk = "sig%d" % si
                    act(sg[:, 0:QN], psg[:, 0:QN], AF.Sigmoid, [pkg, "bgT"], [sgk], bias=bgT[:, l, b * 8 + j:b * 8 + j + 1])
                    yk = "ysum%d" % j
                    if b == 0:
                        tt(ysum[:, j, 0:QN], sg[:, 0:QN], psb[:, 0:QN], ALU.mult, [sgk, pkb], [yk])
                    elif b == 1:
                        tt(sg[:, 0:QN], sg[:, 0:QN], psb[:, 0:QN], ALU.mult, [sgk, pkb], [sgk])
                        tt(ysum[:, j, 0:QN], ysum[:, j, 0:QN], sg[:, 0:QN], ALU.add, [yk, sgk], [yk])
                    else:
                        tt(sg[:, 0:QN], sg[:, 0:QN], psb[:, 0:QN], ALU.mult, [sgk, pkb], [sgk])
                        tt(yT[:, j, 0:QN], ysum[:, j, 0:QN], sg[:, 0:QN], ALU.add, [yk, sgk], ["yT"])
        for ct in range(2):
            wt, wk = wtile("wout", l, 0, 8, ct * 512, 512)
            for i in range(nsub):
                ps, pk = mm_bank()
                for kc in range(8):
                    mm(ps[:], yT[:, kc, i * 128:(i + 1) * 128], wt[:, kc, :], kc == 0, kc == 7, ["yT", wk], [pk])
                tt(zt[0][:], ps[:], g1b[:, ct * 512:(ct + 1) * 512], ALU.mult, [pk, "g1b"], ["zt0"])
                xs = xbuf[:, sub0 + i, ct * 512:(ct + 1) * 512]
                tt(xs, xs, zt[0][:], ALU.add, [xk[i], "zt0"], [xk[i]])
        for i in range(nsub):
            rms_hT(xbuf[:, sub0 + i, :], xk[i], l, 1, mslot, i * 128)
        for t in range(6):
            ncol = 512 if t < 5 else 256
            wgt, wgk = wtile("wfi", l, 0, 8, t * 512, ncol)
            wut, wuk = wtile("wfi", l, 0, 8, FFN + t * 512, ncol)
            for j4 in range(ncol // 128):
                j = t * 4 + j4
                psg, pkg = mm_bank()
                for kc in range(8):
                    mm(psg[:, 0:QN], wgt[:, kc, j4 * 128:(j4 + 1) * 128], hT[:, kc, 0:QN], kc == 0, kc == 7, [wgk, "hT"], [pkg])
                psu, pku = mm_bank()
                for kc in range(8):
                    mm(psu[:, 0:QN], wut[:, kc, j4 * 128:(j4 + 1) * 128], hT[:, kc, 0:QN], kc == 0, kc == 7, [wuk, "hT"], [pku])
                si = nxt("sg", 2)
                sg = sig[si]
                sgk = "sig%d" % si
                act(sg[:, 0:QN], psg[:, 0:QN], AF.Silu, [pkg], [sgk])
                tt(hid[:, j, 0:QN], sg[:, 0:QN], psu[:, 0:QN], ALU.mult, [sgk, pku], HIDK)
        for ct in range(2):
            for (k0, nkc) in ((0, 8), (8, 8), (16, 6)):
                wt, wk = wtile("wfo", l, k0, nkc, ct * 512, 512)
                for i in range(nsub):
                    for kk in range(nkc):
                        kc = k0 + kk
                        mm(pacc[i][:], hid[:, kc, i * 128:(i + 1) * 128], wt[:, kk, :], kc == 0, kc == 21, HIDK + [wk], ["pacc%d" % i])
            for i in range(nsub):
                tt(zt[0][:], pacc[i][:], g2b[:, ct * 512:(ct + 1) * 512], ALU.mult, ["pacc%d" % i, "g2b"], ["zt0"])
                xs = xbuf[:, sub0 + i, ct * 512:(ct + 1) * 512]
                tt(xs, xs, zt[0][:], ALU.add, [xk[i], "zt0"], [xk[i]])

    rot.update({"k2": 0, "v2": 0, "sg": 0})

    for s in range(nseq):
        for t in range(16):
            dma_sp(x_sb[:, t, :], x_d[s, t * 128:(t + 1) * 128, :], "xl", [], ["x%d" % t])
        for t in range(2):
            dma_sp(xc_sb[:, t, :], ctx_d[s, t * 128:(t + 1) * 128, :], "xl", [], ["xc%d" % t])
        _last = P.ins[-1]
        for t in range(16):
            P.R("x%d" % t).w = _last
        for t in range(2):
            P.R("xc%d" % t).w = _last
        for l in range(depth):
            first = (s == 0)
            dma_sp(ggq[:], ggq_d[l], "c13", (), ["gvec"])
            dma_sp(ggk[:], ggk_d[l], "c14", (), ["gvec"])
            dma_sp(gmq[:], gmq_d[l], "c15", (), ["gvec"])
            dma_sp(gmkv[:], gmkv_d[l], "c16", (), ["gvec"])
            if first and l == 0:
                drain_casts(0, 36)
            phase_a_tile(l, xc_sb, "xc", 0, 2, 0, False, 0, 2)
            for q in range(8):
                if first and l == 0:
                    drain_casts(0, 35)
                phase_a_tile(l, x_sb, "x", q * 2, 2, CTX + q * 256, True, q * 2, s)
            if first and l == 0:
                drain_casts(0, 1000)
            if l < depth - 1:
                bcast_gates(l, 2)
                phase_b_tile(l, xc_sb, "xc", 0, 2, False, 0, 2, 2)
            bcast_gates(l, s)
            for q in range(SEQ // QT):
                phase_b_tile(l, x_sb, "x", q * NS, NS, True, q * NS, s, NKEY // 128)
                if first and l == 0 and depth > 1:
                    drain_casts(1, 40 if q < 7 else 1000)
        dma_sp(g2b[:], gfb_d, "c7", (), ["g2b"])
        for t in range(16):
            act(junk[:], x_sb[:, t, :], AF.Square, ["x%d" % t], ["junk", "st4"], accum_out=st[:, 4:5])
            ts(st[:, 5:6], st[:, 4:5], 1.0 / D, EPS, ALU.mult, ALU.add, ["st4"], ["st5"])
            rsqrt_small(st[:, 5:6], 1, "st5")
            oi = nxt("os", 2)
            ok_ = ["ysum%d" % (4 * oi + q_) for q_ in range(4)]
            stt(outst[oi], x_sb[:, t, :], st[:, 5:6], g2b[:], ALU.mult, ALU.mult, ["x%d" % t, "st5", "g2b"], ok_)
            dma_sp(out_d[s, t * 128:(t + 1) * 128, :], outst[oi], "o%d" % oi, ok_, [], is_out=True)
    P.finalize()
    return nc, P


_IN_SPLITS = np.cumsum([512, 512, 512, 512, 128, 128, 384, 256, 32, 3072])


def _tables():
    t = np.arange(SEQ)
    row = (t // 64).astype(np.float32)
    col = (t % 64).astype(np.float32)

    def tab(dim):
        a = dim // 2
        fr = (THETA ** (-np.arange(0, a, 2, dtype=np.float32) / a)).astype(np.float32)
        ar = row[:, None] * fr
        ac = col[:, None] * fr
        c = np.concatenate([np.cos(ar), np.cos(ac)], 1).astype(np.float32)
        s_ = np.concatenate([np.sin(ar), np.sin(ac)], 1).astype(np.float32)
        n = c.shape[1]
        return (np.ascontiguousarray(c.reshape(16, 128, n).transpose(1, 0, 2)),
                np.ascontiguousarray(s_.reshape(16, 128, n).transpose(1, 0, 2)))
    return tab(64), tab(32)


def _prep_shared(inp):
    f = np.float32
    w_in = inp["w_in"]
    sp = _IN_SPLITS
    dq, dk, dv, gq, gk, gv, mcq, mckv, mkr, gt = np.split(w_in, sp[:-1], axis=-1)
    sh = {}
    sh["w_mod"] = np.ascontiguousarray(inp["w_mod"], f)
    sh["b_mod3"] = np.ascontiguousarray(np.broadcast_to(inp["b_mod"][:, None, :], (2, 3, 6 * D)), f)
    tr = lambda v: np.ascontiguousarray(v.reshape(2, -1, 128).transpose(0, 2, 1), f)
    sh["g_norm1T"] = tr(inp["g_norm1"])
    sh["g_norm2T"] = tr(inp["g_norm2"])
    sh["wA"] = np.ascontiguousarray(np.concatenate([dk, dv, gk, gv, mckv, mkr], -1), f)
    sh["wQ"] = np.ascontiguousarray(np.concatenate([dq, gq, mcq], -1), f)
    sh["wG"] = np.ascontiguousarray(gt, f)
    sh["b_gateT"] = tr(inp["b_gate"])
    lam = np.stack([inp["lam_q1"], inp["lam_k1"], inp["lam_q2"], inp["lam_k2"]], 1)
    sh["lamb"] = np.ascontiguousarray(np.broadcast_to(lam[:, :, None, :], (2, 4, 128, 64)), f)
    sh["g_diff_outT"] = np.ascontiguousarray(inp["g_diff_out"].reshape(2, 128, 1), f)
    bc = lambda v: np.ascontiguousarray(np.broadcast_to(v[:, None, :], (2, 128, v.shape[-1])), f)
    sh["g_gqa_qb"] = bc(inp["g_gqa_q"])
    sh["g_gqa_kb"] = bc(inp["g_gqa_k"])
    sh["g_mla_qb"] = bc(inp["g_mla_q"])
    sh["g_mla_kvb"] = bc(inp["g_mla_kv"])
    sh["w_uq"] = np.ascontiguousarray(inp["w_mla_uq"], f)
    ukv = inp["w_mla_ukv"].reshape(2, 256, 8, 128)
    sh["w_ukv2"] = np.ascontiguousarray(np.concatenate([ukv[..., :64].reshape(2, 256, 512), ukv[..., 64:].reshape(2, 256, 512)], -1), f)
    sh["w_br0"] = np.ascontiguousarray(inp["w_br_diff"], f)
    sh["w_br1"] = np.ascontiguousarray(inp["w_br_gqa"], f)
    sh["w_br2"] = np.ascontiguousarray(inp["w_br_mla"], f)
    sh["w_out"] = np.ascontiguousarray(inp["w_out"], f)
    sh["w_ffn_in"] = np.ascontiguousarray(inp["w_ffn_in"], f)
    sh["w_ffn_out"] = np.ascontiguousarray(inp["w_ffn_out"], f)
    sh["g_finalb"] = np.ascontiguousarray(np.broadcast_to(inp["g_final"][None, :], (128, D)), f)
    sh["ident"] = np.eye(128, dtype=f)
    sel = np.zeros((3, 3, 128), f)
    for k in range(3):
        sel[k, k, :] = 1.0
    sh["sel3"] = sel
    (qc, qs), (mc, ms) = _tables()
    sh["tab_qk_cos"], sh["tab_qk_sin"], sh["tab_m_cos"], sh["tab_m_sin"] = qc, qs, mc, ms
    return sh


def _core_inputs(inp, sh, seqs):
    f = np.float32
    m = dict(sh)
    m["x"] = np.ascontiguousarray(inp["x"][seqs], f)
    m["ctx"] = np.ascontiguousarray(inp["ctx"][seqs], f)
    cs = [inp["c"][s] for s in seqs]
    while len(cs) < 2:
        cs.append(cs[0])
    c3 = np.stack(cs + [inp["c_ctx"]], 0)
    m["cT"] = np.ascontiguousarray(c3.reshape(3, 8, 128).transpose(2, 1, 0), f)
    return m


_CACHE = {}


def kernel(**inputs):
    inp = {k: np.asarray(v) for k, v in inputs.items()}
    ncores = 8
    nseq = 2
    if "prog" not in _CACHE:
        _CACHE["prog"] = build_program(nseq=nseq, depth=2)[0]
    nc = _CACHE["prog"]
    sh = _prep_shared(inp)
    in_maps = [_core_inputs(inp, sh, list(range(c * nseq, (c + 1) * nseq))) for c in range(ncores)]
    res = run_bass_kernel_spmd(nc, in_maps, core_ids=list(range(ncores)))
    out = np.concatenate([np.asarray(r["out"], np.float32) for r in res.results], axis=0)
    return out
```

```python
import math
import numpy as np
import concourse.bass as bass
import concourse.mybir as mybir
from concourse.bass_utils import run_bass_kernel_spmd

F32 = mybir.dt.float32
BF16 = mybir.dt.bfloat16
AF = mybir.ActivationFunctionType
ALU = mybir.AluOpType
AX = mybir.AxisListType

D = 1024
SEQ = 2048
CTX = 256
NKEY = SEQ + CTX
EPS = 1e-6
FFN = 2816
THETA = 10000.0
WA_COLS = 1568
WQ_COLS = 1408


class Res:
    __slots__ = ("name", "w", "r")

    def __init__(self, name):
        self.name = name
        self.w = None
        self.r = {}


class Ins:
    __slots__ = ("eng", "fn", "deps", "sig", "tok", "dma", "dsem")

    def __init__(self, eng, fn, dma=False):
        self.eng = eng
        self.fn = fn
        self.deps = []
        self.sig = False
        self.tok = None
        self.dma = dma
        self.dsem = None


class Prog:
    def __init__(self, nc):
        self.nc = nc
        self.E = {"pe": nc.tensor, "act": nc.scalar, "dve": nc.vector, "pool": nc.gpsimd, "sp": nc.sync}
        self.ins = []
        self.res = {}
        self.dsems = {}
        self.outs = []

    def R(self, name):
        r = self.res.get(name)
        if r is None:
            r = self.res[name] = Res(name)
        return r

    def _track(self, I, reads, writes):
        deps = I.deps

        def add(d, raw):
            if d is None or d is I:
                return
            if (not d.dma) and (not I.dma) and d.eng == I.eng and not raw:
                return
            if (not d.dma) and (not I.dma) and d.eng == I.eng == "pe":
                return
            deps.append(d)

        for r in reads:
            r = self.R(r) if isinstance(r, str) else r
            add(r.w, True)
        for w in writes:
            w = self.R(w) if isinstance(w, str) else w
            add(w.w, False)
            for rd in w.r.values():
                add(rd, False)
        rk = ("d", I.dsem) if I.dma else I.eng
        for r in reads:
            r = self.R(r) if isinstance(r, str) else r
            r.r[rk] = I
        for w in writes:
            w = self.R(w) if isinstance(w, str) else w
            w.w = I
            w.r = {}

    def op(self, eng, fn, reads=(), writes=()):
        I = Ins(eng, fn)
        self._track(I, reads, writes)
        self.ins.append(I)
        return I

    def dma(self, q, fn, semkey, reads=(), writes=(), out=False):
        I = Ins(q, fn, dma=True)
        I.dsem = semkey
        self._track(I, reads, writes)
        self.ins.append(I)
        if out:
            self.outs.append(I)
        return I

    def finalize(self):
        nc = self.nc
        for I in self.ins:
            for d in I.deps:
                d.sig = True
        esem = {e: nc.alloc_semaphore(name="es_" + e) for e in self.E}
        cnt = {e: 0 for e in self.E}
        dcnt = {}
        waited = {e: {} for e in self.E}
        semid = {}

        def wait_on(engname, need):
            eng = self.E[engname]
            wd = waited[engname]
            for key, (sem, val) in need.items():
                if wd.get(key, 0) >= val:
                    continue
                eng.wait_ge(sem, val)
                wd[key] = val

        for I in self.ins:
            need = {}
            for d in I.deps:
                sem, val, key = d.tok
                if key not in need or need[key][1] < val:
                    need[key] = (sem, val)
            wait_on(I.eng, need)
            inst = I.fn(self.E[I.eng])
            if I.dma:
                k = I.dsem
                if k not in self.dsems:
                    self.dsems[k] = nc.alloc_semaphore(name="ds_" + k)
                    dcnt[k] = 0
                dcnt[k] += 16
                inst.then_inc(self.dsems[k], 16)
                I.tok = (self.dsems[k], dcnt[k], "d_" + k)
            elif I.sig:
                cnt[I.eng] += 1
                inst.then_inc(esem[I.eng], 1)
                I.tok = (esem[I.eng], cnt[I.eng], "e_" + I.eng)
        need = {}
        for d in self.outs:
            sem, val, key = d.tok
            if key not in need or need[key][1] < val:
                need[key] = (sem, val)
        wait_on("sp", need)
        need = {}
        for e in ("pe", "act", "dve", "pool"):
            if cnt[e] > 0:
                need["e_" + e] = (esem[e], cnt[e])
        wait_on("sp", need)
        self.counts = dict(cnt)


def fview(t, p0, npart, off, dims):
    ps = t[:].ap[0][0]
    return bass.AP(t, p0 * ps + off, [[ps, npart]] + [[s, c] for (s, c) in dims])


def build_program(nseq=2, depth=2, dbg=None):
    nc = bass.Bass("TRN2", target_bir_lowering=False)
    P = Prog(nc)
    QT = 256
    NS = QT // 128

    def din(name, shape, dt=F32):
        return nc.dram_tensor(name, list(shape), dt, kind="ExternalInput").ap()

    x_d = din("x", [nseq, SEQ, D])
    ctx_d = din("ctx", [nseq, CTX, D])
    cT_d = din("cT", [128, 8, 3])
    wmod_d = din("w_mod", [2, D, 6 * D])
    bmod_d = din("b_mod3", [2, 3, 6 * D])
    gn1_d = din("g_norm1T", [2, 128, 8])
    gn2_d = din("g_norm2T", [2, 128, 8])
    wA_d = din("wA", [2, D, WA_COLS])
    wQ_d = din("wQ", [2, D, WQ_COLS])
    wG_d = din("wG", [2, D, 3 * D])
    bg_d = din("b_gateT", [2, 128, 24])
    lam_d = din("lamb", [2, 4, 128, 64])
    gdo_d = din("g_diff_outT", [2, 128, 1])
    ggq_d = din("g_gqa_qb", [2, 128, 64])
    ggk_d = din("g_gqa_kb", [2, 128, 64])
    gmq_d = din("g_mla_qb", [2, 128, 384])
    gmkv_d = din("g_mla_kvb", [2, 128, 256])
    wuq_d = din("w_uq", [2, 384, 768])
    wukv_d = din("w_ukv2", [2, 256, 1024])
    wbr_d = [din("w_br%d" % b, [2, 512, D]) for b in range(3)]
    wout_d = din("w_out", [2, D, D])
    wfi_d = din("w_ffn_in", [2, D, 2 * FFN])
    wfo_d = din("w_ffn_out", [2, FFN, D])
    gfb_d = din("g_finalb", [128, D])
    ident_d = din("ident", [128, 128])
    sel_d = din("sel3", [3, 3, 128])
    tqc_d = din("tab_qk_cos", [128, 16, 32])
    tqs_d = din("tab_qk_sin", [128, 16, 32])
    tmc_d = din("tab_m_cos", [128, 16, 16])
    tms_d = din("tab_m_sin", [128, 16, 16])
    out_d = nc.dram_tensor("out", [nseq, SEQ, D], F32, kind="ExternalOutput").ap()
    KdT_d = nc.dram_tensor("KdT", [4, 128, NKEY], BF16, kind="Internal").ap()
    KgT_d = nc.dram_tensor("KgT", [128, NKEY], BF16, kind="Internal").ap()
    KmT_d = nc.dram_tensor("KmT", [8, 96, NKEY], BF16, kind="Internal").ap()
    Vd_d = nc.dram_tensor("Vd", [NKEY, 512], BF16, kind="Internal").ap()
    Vg_d = nc.dram_tensor("Vg", [NKEY, 128], BF16, kind="Internal").ap()
    Vm_d = nc.dram_tensor("Vm", [NKEY, 512], BF16, kind="Internal").ap()
    modg_d = nc.dram_tensor("modg_scr", [2, 2, 3, D], F32, kind="Internal").ap()
    WSRC = {"wA": wA_d, "wukv": wukv_d, "wQ": wQ_d, "wuq": wuq_d, "wG": wG_d, "wbr0": wbr_d[0], "wbr1": wbr_d[1],
            "wbr2": wbr_d[2], "wout": wout_d, "wfi": wfi_d, "wfo": wfo_d}
    WT = {
        "wA": [(0, 8, 0, 512), (0, 8, 512, 512), (0, 8, 1024, 512), (0, 8, 1536, 32)],
        "wukv": [(0, 2, 0, 512), (0, 2, 512, 512)],
        "wQ": [(0, 8, 0, 512), (0, 8, 512, 512), (0, 8, 1024, 384)],
        "wuq": [(0, 3, 0, 384), (0, 3, 384, 384)],
        "wG": [(0, 8, c * 512, 512) for c in range(6)],
        "wbr0": [(0, 4, 0, 512), (0, 4, 512, 512)], "wbr1": [(0, 4, 0, 512), (0, 4, 512, 512)],
        "wbr2": [(0, 4, 0, 512), (0, 4, 512, 512)],
        "wout": [(0, 8, 0, 512), (0, 8, 512, 512)],
        "wfi": [(0, 8, t * 512, 512 if t < 5 else 256) for t in range(6)] + [(0, 8, FFN + t * 512, 512 if t < 5 else 256) for t in range(6)],
        "wfo": [(k0, nkc, ct * 512, 512) for ct in range(2) for (k0, nkc) in ((0, 8), (8, 8), (16, 6))],
    }
    WOFF = {}
    WSCR = {}
    cast_jobs = {0: [], 1: []}
    cast_res = {}
    for nm in ("wA", "wukv", "wQ", "wuq", "wG", "wbr0", "wbr1", "wbr2", "wout", "wfi", "wfo"):
        off = 0
        for tl in WT[nm]:
            WOFF[(nm,) + tl] = off
            off += 128 * tl[1] * tl[3]
        WSCR[nm] = nc.dram_tensor(nm + "_b", [2, off], BF16, kind="Internal").ap()
    for l_ in range(2):
        for nm in ("wA", "wukv", "wQ", "wuq", "wG", "wbr0", "wbr1", "wbr2", "wout", "wfi", "wfo"):
            src_ = WSRC[nm]
            scr = WSCR[nm]
            tot = scr.shape[1]
            for tl in WT[nm]:
                k0, kc, c0, ncols = tl
                keys = []
                base = l_ * tot + WOFF[(nm,) + tl]
                for kk in range(kc):
                    if nm == "wbr1":
                        parts = [(kv_ * 64, 64, kv_ * 256 + (k0 + kk) * 64) for kv_ in range(2)]
                    else:
                        parts = [(0, 128, (k0 + kk) * 128)]
                    for (p0, npp, r0) in parts:
                        k_ = "cw_%s_%d_%d_%d_%d_%d" % (nm, l_, k0, c0, kk, p0)
                        keys.append(k_)
                        dst_ = bass.AP(scr.tensor, base + p0 * kc * ncols + kk * ncols, [[kc * ncols, npp], [1, ncols]])
                        cast_jobs[l_].append((dst_, src_[l_, r0:r0 + npp, c0:c0 + ncols], "cs_%s_%d" % (nm, l_), k_))
                cast_res.setdefault((nm, l_), []).extend(keys)

    def drain_casts(l_, n):
        for _ in range(min(n, len(cast_jobs[l_]))):
            dst_, src_, sem_, k_ = cast_jobs[l_].pop(0)
            P.dma("pool", (lambda e, dst_=dst_, src_=src_: e.dma_start(out=dst_, in_=src_)), sem_, (), [k_])

    dbg_d = None
    if dbg is not None:
        dbg_d = nc.dram_tensor("dbg", [128, dbg], F32, kind="ExternalOutput").ap()

    def sb(name, shape, dt):
        return nc.alloc_sbuf_tensor("s_" + name, shape, dt)
    x_sb = sb("x_sb", [128, 16, D], F32)
    xc_sb = sb("xc_sb", [128, 2, D], F32)
    ident_f = sb("ident_f", [128, 128], F32)
    ident_b = sb("ident_b", [128, 128], BF16)
    ones_b = sb("ones_b", [128, 128], BF16)
    tqc = sb("tqc", [128, 16, 32], F32)
    tqs = sb("tqs", [128, 16, 32], F32)
    tmc = sb("tmc", [128, 16, 16], F32)
    tms = sb("tms", [128, 16, 16], F32)
    cT = sb("cT", [128, 8, 3], F32)
    modT = sb("modT", [128, 2, 48, 3], F32)
    gsv = sb("gsv", [128, 2, 2, 8, 3], F32)
    gn = sb("gn", [128, 2, 2, 8], F32)
    bgT = sb("bgT", [128, 2, 24], F32)
    lamb = sb("lamb", [128, 4, 64], F32)
    lamv = sb("lamv", [128, 2, 4], F32)
    gdo = sb("gdo", [128, 2, 1], F32)
    ggq = sb("ggq", [128, 64], F32)
    ggk = sb("ggk", [128, 64], F32)
    gmq = sb("gmq", [128, 384], F32)
    gmkv = sb("gmkv", [128, 256], F32)
    g1b = sb("g1b", [128, D], F32)
    g2b = sb("g2b", [128, D], F32)
    NW = 4
    wring = [sb("wring%d" % i, [128, 8, 512], BF16) for i in range(NW)]
    hT = sb("hT", [128, 8, 256], BF16)
    junk = sb("junk", [128, D], BF16)
    st = sb("st", [128, 16], F32)
    diag = sb("diag", [128, 128], F32)
    zt = [sb("zt%d" % i, [128, 512], F32) for i in range(4)]
    qtm = sb("qtm", [128, 768], BF16)
    vtm = sb("vtm", [128, 512], BF16)
    kstage = sb("kstage", [128, 8, 128], BF16)
    QdT = sb("QdT", [128, 8, QT], BF16)
    QgT = sb("QgT", [128, 8, QT], BF16)
    QmT = sb("QmT", [128, 8, QT], BF16)
    cqnT = sb("cqnT", [128, 3, QT], BF16)
    kvh = sb("kvh", [128, 4 * NKEY], BF16)
    kbuf = [kvh[:, i * NKEY:(i + 1) * NKEY] for i in range(2)]
    vbuf = [kvh[:, (2 + i) * NKEY:(3 + i) * NKEY].rearrange("p (a b) -> p a b", a=18) for i in range(2)]
    hid = kvh[:, 0:22 * QT].rearrange("p (a b) -> p a b", a=22)
    HIDK = ["kbuf0", "kbuf1", "vbuf0"]
    PT = [sb("PT%d" % i, [128, 512], BF16) for i in range(4)]
    OTall = sb("OTall", [128, 12, QT], BF16)
    OT = [OTall[:, 4 * b:4 * b + 4, :] for b in range(3)]
    h2T = OTall[:, 0:8, :]
    H2K = ["OT0", "OT1"]
    of32 = sb("of32", [128, QT], F32)
    sqb = sb("sqb", [128, QT], BF16)
    ysum = sb("ysum", [128, 8, QT], F32)
    yT = sb("yT", [128, 8, QT], BF16)
    outst = [ysum[:, 4 * i:4 * i + 4, :].rearrange("p a b -> p (a b)") for i in range(2)]
    sig = [sb("sig%d" % i, [128, QT], F32) for i in range(2)]
    pmm = [nc.alloc_psum_tensor("pmm%d" % i, [128, 512], F32) for i in range(3)]
    psc = [nc.alloc_psum_tensor("psc%d" % i, [128, 512], F32) for i in range(3)]
    pacc = [nc.alloc_psum_tensor("pacc%d" % i, [128, 512], F32) for i in range(2)]

    rot = {"mm": 0, "sc": 0, "w": 0, "pt": 0, "k": 0, "v": 0, "os": 0, "wm": 0, "zt": 0}

    def nxt(key, n):
        i = rot[key]
        rot[key] = (i + 1) % n
        return i

    def mm_bank():
        i = nxt("mm", 3)
        return pmm[i], "pmm%d" % i

    def dma_sp(out, in_, key, reads=(), writes=(), is_out=False):
        return P.dma("sp", lambda e: e.dma_start(out=out, in_=in_, allow_slow_non_contiguous=True), key, reads, writes, out=is_out)

    def dma_pool(out, in_, key, reads=(), writes=()):
        return P.dma("pool", lambda e: e.dma_start(out=out, in_=in_, allow_slow_non_contiguous=True), key, reads, writes)

    def act(out, in_, func, reads, writes, **kw):
        return P.op("act", lambda e: e.activation(out=out, in_=in_, func=func, **kw), reads, writes)

    def tt(out, in0, in1, op, reads, writes):
        return P.op("dve", lambda e: e.tensor_tensor(out=out, in0=in0, in1=in1, op=op), reads, writes)

    def ts(out, in0, s1, s2, op0, op1, reads, writes):
        if op1 is None:
            return P.op("dve", lambda e: e.tensor_scalar(out=out, in0=in0, scalar1=s1, scalar2=None, op0=op0), reads, writes)
        return P.op("dve", lambda e: e.tensor_scalar(out=out, in0=in0, scalar1=s1, scalar2=s2, op0=op0, op1=op1), reads, writes)

    def stt(out, in0, scalar, in1, op0, op1, reads, writes):
        return P.op("dve", lambda e: e.scalar_tensor_tensor(out=out, in0=in0, scalar=scalar, in1=in1, op0=op0, op1=op1), reads, writes)

    def cp(out, in_, reads, writes):
        return P.op("dve", lambda e: e.tensor_copy(out=out, in_=in_), reads, writes)

    def mm(out, lhsT, rhs, start, stop, reads, writes):
        return P.op("pe", lambda e: e.matmul(out, lhsT=lhsT, rhs=rhs, start=start, stop=stop), reads, writes)

    def tp(out, in_, reads, writes, f32=False):
        idn = ident_f if f32 else ident_b
        np_ = in_.shape[0]
        return P.op("pe", lambda e: e.transpose(out, in_, idn[0:np_, 0:np_]), list(reads) + ["ident"], writes)

    def wtile(nm, l_, k0, kc, c0, ncols):
        i = nxt("w", NW)
        t = wring[i]
        scr = WSCR[nm]
        base = l_ * scr.shape[1] + WOFF[(nm, k0, kc, c0, ncols)]
        src = bass.AP(scr.tensor, base, [[kc * ncols, 128], [ncols, kc], [1, ncols]])
        dma_pool(t[:, 0:kc, 0:ncols], src, "w%d" % i, cast_res[(nm, l_)], ["w%d" % i])
        return t, "w%d" % i

    def rsqrt_small(ap, n, key):
        act(ap, ap, AF.Ln, [key], [key])
        act(ap, ap, AF.Exp, [key], [key], scale=-0.5)

    dma_sp(ident_f[:], ident_d, "c0", (), ["ident"])
    dma_pool(ident_b[:], ident_d, "c1", (), ["ident"])
    P.op("dve", lambda e: e.memset(ones_b[:], 1.0), (), ["ones"])
    P.op("dve", lambda e: e.memset(QdT[:], 0.0), (), ["QdT"])
    P.op("dve", lambda e: e.memset(QgT[:], 0.0), (), ["QgT"])
    P.op("dve", lambda e: e.memset(QmT[:], 0.0), (), ["QmT"])
    dma_sp(tqc[:], tqc_d, "c3", (), ["tab"])
    dma_sp(tqs[:], tqs_d, "c4", (), ["tab"])
    dma_sp(tmc[:], tmc_d, "c5", (), ["tab"])
    dma_sp(tms[:], tms_d, "c6", (), ["tab"])
    dma_sp(cT[:], cT_d, "c8", (), ["cT"])
    dma_sp(gn[:, :, 0, :], gn1_d.rearrange("l p c -> p l c"), "c9", (), ["gn"])
    dma_sp(gn[:, :, 1, :], gn2_d.rearrange("l p c -> p l c"), "c10", (), ["gn"])
    dma_sp(bgT[:], bg_d.rearrange("l p c -> p l c"), "c11", (), ["bgT"])
    dma_sp(gdo[:], gdo_d.rearrange("l p c -> p l c"), "c12", (), ["gdo"])
    act(cT[:], cT[:], AF.Silu, ["cT"], ["cT"])

    lam_init = [0.8 - 0.6 * math.exp(-0.3 * l) for l in range(2)]
    for l in range(depth):
        dma_sp(lamb[:], lam_d[l].rearrange("k p c -> p k c"), "c17", (), ["lamb"])
        tt(zt[0][:, 0:64], lamb[:, 0, :], lamb[:, 1, :], ALU.mult, ["lamb"], ["zt0"])
        tt(zt[0][:, 64:128], lamb[:, 2, :], lamb[:, 3, :], ALU.mult, ["lamb"], ["zt0"])
        P.op("dve", lambda e: e.tensor_reduce(out=st[:, 0:2], in_=zt[0][:, 0:128].rearrange("p (a b) -> p a b", a=2), axis=AX.X, op=ALU.add), ["zt0"], ["st"])
        act(st[:, 0:2], st[:, 0:2], AF.Exp, ["st"], ["st"])
        tt(st[:, 2:3], st[:, 1:2], st[:, 0:1], ALU.subtract, ["st"], ["st"])
        _l = l
        ts(lamv[:, l, 0:1], st[:, 2:3], -lam_init[l], None, ALU.add, None, ["st"], ["lamv"])
        ts(gdo[:, l, :], gdo[:, l, :], 1.0 - lam_init[l], None, ALU.mult, None, ["gdo"], ["gdo"])
        for cc in range(24):
            i = nxt("w", NW)
            wt = wring[i][:].rearrange("p a b -> p (a b)").bitcast(F32).rearrange("p (a b) -> p a b", a=8)
            wmk = "w%d" % i
            dma_sp(wt, wmod_d[l, :, cc * 256:(cc + 1) * 256].rearrange("(kc p) n -> p kc n", p=128), wmk, (), [wmk])
            bmt = zt[1]
            dma_sp(bmt[0:3, 0:256], bmod_d[l, :, cc * 256:(cc + 1) * 256], "bm", (), ["zt1"])
            ps, pk = mm_bank()
            for kc in range(8):
                mm(ps[0:3, 0:256], cT[:, kc, :], wt[:, kc, :], kc == 0, kc == 7, ["cT", wmk], [pk])
            tt(bmt[0:3, 0:256], ps[0:3, 0:256], bmt[0:3, 0:256], ALU.add, [pk, "zt1"], ["zt1"])
            which = cc // 4
            if which in (2, 5):
                gi = 0 if which == 2 else 1
                dma_sp(modg_d[l, gi, :, (cc % 4) * 256:(cc % 4) * 256 + 256], bmt[0:3, 0:256], "mg", ["zt1"], ["modg%d" % l])
            else:
                ps2, pk2 = mm_bank()
                for q in range(2):
                    mm(ps2[:, q * 3:(q + 1) * 3], bmt[0:3, q * 128:(q + 1) * 128], ident_f[0:3, 0:3], True, True, ["zt1", "ident"], [pk2])
                cp(modT[:, l, cc * 2:(cc + 1) * 2, :], ps2[:, 0:6].rearrange("p (a b) -> p a b", a=2), [pk2], ["modT"])
        for ni, wh in ((0, 1), (1, 4)):
            for s in range(3):
                stt(gsv[:, l, ni, :, s], modT[:, l, wh * 8:(wh + 1) * 8, s], 1.0, gn[:, l, ni, :], ALU.add, ALU.mult, ["modT", "gn"], ["gsv"])

    def bcast_gates(l, s):
        for gi, gt, gk in ((0, g1b, "g1b"), (1, g2b, "g2b")):
            src = modg_d[l, gi, s:s + 1, :]
            srcb = bass.AP(src.tensor, src.offset, [[0, 128], [1, D]])
            dma_sp(gt[:], srcb, "gb%d" % gi, ["modg%d" % l], [gk])

    def rms_hT(xap, xres, l, ni, s, col0, dst=None, dkeys=("hT",)):
        dst = hT if dst is None else dst
        act(junk[:], xap, AF.Square, [xres], ["junk", "st4"], accum_out=st[:, 4:5])
        ts(st[:, 5:6], st[:, 4:5], 1.0 / D, EPS, ALU.mult, ALU.add, ["st4"], ["st5"])
        rsqrt_small(st[:, 5:6], 1, "st5")
        ts(diag[:], ident_f[:], st[:, 5:6], None, ALU.mult, None, ["ident", "st5"], ["diag"])
        banks = [mm_bank(), mm_bank()]
        for c in range(8):
            ps, pk = banks[c // 4]
            mm(ps[:, (c % 4) * 128:(c % 4 + 1) * 128], xap[:, c * 128:(c + 1) * 128], diag[:], True, True, [xres, "diag"], [pk])
        wh = 0 if ni == 0 else 3
        for c in range(8):
            ps, pk = banks[c // 4]
            act(dst[:, c, col0:col0 + 128], ps[:, (c % 4) * 128:(c % 4 + 1) * 128], AF.Identity, [pk, "gsv", "modT"], list(dkeys),
                scale=gsv[:, l, ni, c, s:s + 1], bias=modT[:, l, wh * 8 + c, s:s + 1])

    def group_norm(src, ncol, G, gvec, dst_f32, key):
        gs = ncol // G
        sq = zt[3]
        act(sq[:, 0:ncol], src, AF.Square, [key], ["zt3"])
        P.op("dve", lambda e: e.tensor_reduce(out=st[:, 8:8 + G], in_=sq[:, 0:ncol].rearrange("p (g d) -> p g d", g=G), axis=AX.X, op=ALU.add), ["zt3"], ["st8"])
        ts(st[:, 8:8 + G], st[:, 8:8 + G], 1.0 / gs, EPS, ALU.mult, ALU.add, ["st8"], ["st8"])
        rsqrt_small(st[:, 8:8 + G], G, "st8")
        srcv = src.rearrange("p (g d) -> p g d", g=G)
        dstv = dst_f32[:, 0:ncol].rearrange("p (g d) -> p g d", g=G)
        rb = fview(st, 0, 128, 8, [(1, G), (0, gs)])
        tt(dstv, srcv, rb, ALU.mult, [key, "st8"], ["_gn_dst"])
        gb = bass.AP(gvec.tensor, gvec.offset, [list(gvec.ap[0]), [0, G], [1, gs]])
        tt(dstv, dstv, gb, ALU.mult, ["_gn_dst", "gvec"], ["_gn_dst"])

    def rope(src, srckey, H, hd, dst_bf, dstkey, cosap, sinap, nfreq, dst_hstride=None, dst_off=0, src_hstride=None, src_off=0):
        sh = src_hstride or hd
        dh = dst_hstride or hd

        def sv(t, off, hs, half):
            return fview(t, 0, 128, off + half * nfreq, [(hs, H), (2 * nfreq, 2), (1, nfreq)])

        cb = bass.AP(cosap.tensor, cosap.offset, [list(cosap.ap[0]), [0, H], [nfreq, 2], [1, nfreq]])
        sbp = bass.AP(sinap.tensor, sinap.offset, [list(sinap.ap[0]), [0, H], [nfreq, 2], [1, nfreq]])
        n = H * 2 * nfreq
        ta = zt[1]
        tb = zt[2]
        tav = fview(ta, 0, 128, 0, [(2 * nfreq, H), (nfreq, 2), (1, nfreq)])
        tbv = fview(tb, 0, 128, 0, [(2 * nfreq, H), (nfreq, 2), (1, nfreq)])
        x1 = sv(src, src_off, sh, 0)
        x2 = sv(src, src_off, sh, 1)
        tt(tav, x1, cb, ALU.mult, [srckey, "tab"], ["zt1"])
        tt(tbv, x2, sbp, ALU.mult, [srckey, "tab"], ["zt2"])
        tt(sv(dst_bf, dst_off, dh, 0), tav, tbv, ALU.subtract, ["zt1", "zt2"], [dstkey])
        tt(tav, x1, sbp, ALU.mult, [srckey, "tab"], ["zt1"])
        tt(tbv, x2, cb, ALU.mult, [srckey, "tab"], ["zt2"])
        tt(sv(dst_bf, dst_off, dh, 1), tav, tbv, ALU.add, ["zt1", "zt2"], [dstkey])

    def tp_bank():
        ps, pk = mm_bank()
        return ps[:].bitcast(BF16), pk

    def transposes_to(src_bf, srckey, blocks, dst_fn, dstkey):
        for i0 in range(0, len(blocks), 8):
            grp = blocks[i0:i0 + 8]
            ptp, ptk = tp_bank()
            for j, (c0, ncol) in enumerate(grp):
                tp(ptp[0:ncol, j * 128:(j + 1) * 128], src_bf[:, c0:c0 + ncol], [srckey], [ptk])
            ncol = grp[0][1]
            dst = dst_fn(i0, len(grp))
            cp(dst, ptp[0:ncol, 0:len(grp) * 128].rearrange("p (a b) -> p a b", a=len(grp)), [ptk], [dstkey])

    def transposes_pad(src_bf, srckey, Qp, qk, i):
        ptp, ptk = tp_bank()
        for j in range(4):
            tp(ptp[:, j * 128:(j + 1) * 128], src_bf[:, j * 128:(j + 1) * 128], [srckey], [ptk])
        QTn = Qp.shape[2]
        cp(fview(Qp, 0, 64, i * 128, [(2 * QTn, 4), (1, 128)]), ptp[0:64, 0:512].rearrange("p (a b) -> p a b", a=4), [ptk], [qk])
        cp(fview(Qp, 64, 64, QTn + i * 128, [(2 * QTn, 4), (1, 128)]), ptp[64:128, 0:512].rearrange("p (a b) -> p a b", a=4), [ptk], [qk])

    ckvnTa = sb("ckvnTa", [128, 2, 256], BF16)
    krtma = sb("krtma", [128, 2, 32], BF16)
    kmtm = sb("kmtm", [128, 768], BF16)
    kst2 = [kstage, sb("kstage1", [128, 8, 128], BF16)]
    vtm2 = [vtm, sb("vtm1", [128, 512], BF16)]

    def kst():
        i = nxt("k", 2)
        return kst2[i], "kstage%d" % i, "kst%d" % i

    def vst():
        i = nxt("v", 2)
        return vtm2[i], "vtm%d" % i, "vst%d" % i

    def phase_a_tile(l, xbuf, xkey, sub0, nsub, key0, latent, lat_sub0, mslot):
        for i in range(nsub):
            rms_hT(xbuf[:, sub0 + i, :], "%s%d" % (xkey, sub0 + i), l, 0, mslot, i * 128)
        groups = [(0, 512), (512, 512), (1024, 512), (1536, 32)]
        for gi, (c0, ncol) in enumerate(groups):
            wt, wk = wtile("wA", l, 0, 8, c0, ncol)
            for i in range(nsub):
                ks = key0 + i * 128
                kt = ks // 128
                ps, pk = mm_bank()
                for kc in range(8):
                    mm(ps[:, 0:ncol], hT[:, kc, i * 128:(i + 1) * 128], wt[:, kc, 0:ncol], kc == 0, kc == 7, ["hT", wk], [pk])
                tsub = lat_sub0 + i
                if gi == 0:
                    if latent:
                        rope(ps, pk, 8, 64, qtm, "qtm", tqc[:, tsub, :], tqs[:, tsub, :], 16)
                    else:
                        cp(qtm[:, 0:512], ps[:], [pk], ["qtm"])
                    kb_, kbk, ksem = kst()
                    transposes_to(qtm, "qtm", [(j * 128, 128) for j in range(4)], lambda i0, n: kb_[:, 0:4, :], kbk)
                    dma_sp(KdT_d[:, :, ks:ks + 128].rearrange("j p n -> p j n"), kb_[:, 0:4, :], ksem, [kbk], ["KdT_%d" % kt])
                elif gi == 1:
                    vb_, vbk, vsem = vst()
                    cp(vb_[:], ps[:], [pk], [vbk])
                    dma_sp(Vd_d[ks:ks + 128, :], vb_[:], vsem, [vbk], ["Vd_%d" % kt])
                elif gi == 2:
                    group_norm(ps[:, 0:128], 128, 2, ggk[:], zt[0], pk)
                    if latent:
                        rope(zt[0], "_gn_dst", 2, 64, qtm, "qtm", tqc[:, tsub, :], tqs[:, tsub, :], 16)
                    else:
                        cp(qtm[:, 0:128], zt[0][:, 0:128], ["_gn_dst"], ["qtm"])
                    kb_, kbk, ksem = kst()
                    transposes_to(qtm, "qtm", [(0, 128)], lambda i0, n: kb_[:, 0:1, :], kbk)
                    dma_sp(KgT_d[:, ks:ks + 128], kb_[:, 0, :], ksem, [kbk], ["KgT_%d" % kt])
                    vb_, vbk, vsem = vst()
                    cp(vb_[:, 0:128], ps[:, 128:256], [pk], [vbk])
                    dma_sp(Vg_d[ks:ks + 128, :], vb_[:, 0:128], vsem, [vbk], ["Vg_%d" % kt])
                    group_norm(ps[:, 256:512], 256, 1, gmkv[:], zt[0], pk)
                    cp(qtm[:, 0:256], zt[0][:, 0:256], ["_gn_dst"], ["qtm"])
                    transposes_to(qtm, "qtm", [(0, 128), (128, 128)], lambda i0, n: ckvnTa[:, :, i * 128:(i + 1) * 128], "ckvnTa")
                else:
                    if latent:
                        rope(ps, pk, 1, 32, krtma, "krtma", tmc[:, tsub, :], tms[:, tsub, :], 8, dst_off=i * 32)
                    else:
                        cp(krtma[:, i, :], ps[:, 0:32], [pk], ["krtma"])
        wkt, wkk = wtile("wukv", l, 0, 2, 0, 512)
        wvt, wvk = wtile("wukv", l, 0, 2, 512, 512)
        for i in range(nsub):
            ks = key0 + i * 128
            kt = ks // 128
            ps, pk = mm_bank()
            for kc in range(2):
                mm(ps[:], ckvnTa[:, kc, i * 128:(i + 1) * 128], wkt[:, kc, :], kc == 0, kc == 1, ["ckvnTa", wkk], [pk])
            cp(fview(kmtm, 0, 128, 0, [(96, 8), (1, 64)]), ps[:].rearrange("p (h d) -> p h d", h=8), [pk], ["kmtm"])
            cp(fview(kmtm, 0, 128, 64, [(96, 8), (1, 32)]), fview(krtma, 0, 128, i * 32, [(0, 8), (1, 32)]), ["krtma"], ["kmtm"])
            kb_, kbk, ksem = kst()
            transposes_to(kmtm, "kmtm", [(h * 96, 96) for h in range(8)], lambda i0, n: kb_[0:96, :, :], kbk)
            dma_sp(KmT_d[:, :, ks:ks + 128].rearrange("h p n -> p h n"), kb_[0:96, :, :], ksem, [kbk], ["KmT_%d" % kt])
            ps2, pk2 = mm_bank()
            for kc in range(2):
                mm(ps2[:], ckvnTa[:, kc, i * 128:(i + 1) * 128], wvt[:, kc, :], kc == 0, kc == 1, ["ckvnTa", wvk], [pk2])
            vb_, vbk, vsem = vst()
            cp(vb_[:], ps2[:], [pk2], [vbk])
            dma_sp(Vm_d[ks:ks + 128, :], vb_[:], vsem, [vbk], ["Vm_%d" % kt])

    def phase_b_tile(l, xbuf, xkey, sub0, nsub, latent, lat_sub0, mslot, nk):
        QN = nsub * 128
        xk = ["%s%d" % (xkey, sub0 + i) for i in range(nsub)]
        def qphase():
            for i in range(nsub):
                rms_hT(xbuf[:, sub0 + i, :], xk[i], l, 0, mslot, i * 128)
            yield
            for gi, (c0, ncol) in enumerate([(0, 512), (512, 512), (1024, 384)]):
                wt, wk = wtile("wQ", l, 0, 8, c0, ncol)
                for i in range(nsub):
                    ps, pk = mm_bank()
                    for kc in range(8):
                        mm(ps[:, 0:ncol], hT[:, kc, i * 128:(i + 1) * 128], wt[:, kc, 0:ncol], kc == 0, kc == 7, ["hT", wk], [pk])
                    tsub = lat_sub0 + i
                    if gi == 0:
                        if latent:
                            rope(ps, pk, 8, 64, qtm, "qtm", tqc[:, tsub, :], tqs[:, tsub, :], 16)
                        else:
                            cp(qtm[:, 0:512], ps[:], [pk], ["qtm"])
                        transposes_pad(qtm, "qtm", QdT, "QdT", i)
                    elif gi == 1:
                        group_norm(ps[:, 0:512], 512, 8, ggq[:], zt[0], pk)
                        for kv in range(2):
                            if latent:
                                rope(zt[0], "_gn_dst", 4, 64, qtm, "qtm", tqc[:, tsub, :], tqs[:, tsub, :], 16,
                                     dst_hstride=128, dst_off=kv * 64, src_hstride=64, src_off=kv * 256)
                            else:
                                cp(fview(qtm, 0, 128, kv * 64, [(128, 4), (1, 64)]), fview(zt[0], 0, 128, kv * 256, [(64, 4), (1, 64)]), ["_gn_dst"], ["qtm"])
                        transposes_pad(qtm, "qtm", QgT, "QgT", i)
                    else:
                        group_norm(ps[:, 0:384], 384, 1, gmq[:], zt[0], pk)
                        cp(qtm[:, 0:384], zt[0][:, 0:384], ["_gn_dst"], ["qtm"])
                        transposes_to(qtm, "qtm", [(j * 128, 128) for j in range(3)], lambda i0, n: cqnT[:, 0:3, i * 128:(i + 1) * 128], "cqnT")
                yield
            for hh in range(2):
                wt, wk = wtile("wuq", l, 0, 3, hh * 384, 384)
                for i in range(nsub):
                    tsub = lat_sub0 + i
                    ps, pk = mm_bank()
                    for kc in range(3):
                        mm(ps[:, 0:384], cqnT[:, kc, i * 128:(i + 1) * 128], wt[:, kc, 0:384], kc == 0, kc == 2, ["cqnT", wk], [pk])
                    cp(fview(qtm, 0, 128, 0, [(96, 4), (1, 64)]), fview(ps, 0, 128, 0, [(96, 4), (1, 64)]), [pk], ["qtm"])
                    if latent:
                        rope(ps, pk, 4, 32, qtm, "qtm", tmc[:, tsub, :], tms[:, tsub, :], 8, dst_hstride=96, dst_off=64, src_hstride=96, src_off=64)
                    else:
                        cp(fview(qtm, 0, 128, 64, [(96, 4), (1, 32)]), fview(ps, 0, 128, 64, [(96, 4), (1, 32)]), [pk], ["qtm"])
                    transposes_to(qtm, "qtm", [(h * 96, 96) for h in range(4)], lambda i0, n: QmT[0:96, hh * 4:(hh + 1) * 4, i * 128:(i + 1) * 128], "QmT")
                yield

        def attn_post():
            nkt = nk

            def kv_reads(name):
                return ["%s_%d" % (name, t) for t in range(nkt)]

            def load_k(src, npart, rd):
                i = nxt("k2", 2)
                kb_ = kbuf[i]
                dma_sp(kb_[0:npart, 0:nk * 128], src, "kb%d" % i, kv_reads(rd), ["kbuf%d" % i])
                return kb_, "kbuf%d" % i

            def load_v(src, rd):
                i = nxt("v2", 2)
                vb_ = vbuf[i]
                dma_sp(vb_[:, 0:nk, :], src.rearrange("(kc p) d -> p kc d", p=128), "vb%d" % i, kv_reads(rd), ["vbuf%d" % i])
                return vb_, "vbuf%d" % i

            sc_d = 64 ** -0.5
            sc_m = 96 ** -0.5
            steps = []

            def add_head(kget, vget, qap, qkey, co, scale, fin):
                for kc0 in range(0, nk, 2):
                    n2 = min(2, nk - kc0)

                    def score(kc0=kc0, n2=n2):
                        kb_, kbk = kget()
                        vget()
                        si = nxt("sc", 3)
                        ps = psc[si]
                        pk = "psc%d" % si
                        for u in range(n2):
                            kc = kc0 + u
                            mm(ps[:, u * QN:(u + 1) * QN], kb_[:, kc * 128:(kc + 1) * 128], qap, True, True, [kbk, qkey], [pk])
                        pi = nxt("pt", 4)
                        pt = PT[pi]
                        ptk = "PT%d" % pi
                        act(pt[:, 0:n2 * QN], ps[:, 0:n2 * QN], AF.Exp, [pk], [ptk], scale=scale)
                        return pt, ptk

                    def pv(st_, kc0=kc0, n2=n2):
                        pt, ptk = st_
                        vb_, vbk = vget()
                        for u in range(n2):
                            kc = kc0 + u
                            mm(pacc[0][:, co:co + QN], vb_[:, kc, :], pt[:, u * QN:(u + 1) * QN], kc == 0, kc == nk - 1, [vbk, ptk], ["pacc0"])
                            mm(pacc[1][:, co:co + QN], ones_b[:, 0:128], pt[:, u * QN:(u + 1) * QN], kc == 0, kc == nk - 1, ["ones", ptk], ["pacc1"])
                    steps.append((score, pv, fin if kc0 + 2 >= nk else None))

            def lazy(fn):
                box = []

                def get():
                    if not box:
                        box.append(fn())
                    return box[0]
                return get

            def cpa(out, in_, reads, writes):
                return P.op("act", lambda e: e.activation(out=out, in_=in_, func=AF.Copy), reads, writes)

            def fin_pair(OTt, otk, jj):
                def s0():
                    cpa(zt[1][0:64, 0:QN], pacc[1][0:64, 0:QN], ["pacc1"], ["zt1"])
                    cpa(zt[1][64:128, 0:QN], pacc[1][64:128, QN:2 * QN], ["pacc1"], ["zt1"])
                    cpa(zt[0][0:64, 0:QN], pacc[0][0:64, 0:QN], ["pacc0"], ["zt0"])
                    cpa(zt[0][64:128, 0:QN], pacc[0][64:128, QN:2 * QN], ["pacc0"], ["zt0"])
                    P.op("dve", lambda e: e.reciprocal(out=zt[1][:, 0:QN], in_=zt[1][:, 0:QN]), ["zt1"], ["zt1"])
                    tt(OTt[:, jj, 0:QN], zt[0][:, 0:QN], zt[1][:, 0:QN], ALU.mult, ["zt0", "zt1"], [otk])
                return [s0]

            def fin_diff(h):
                box = {}

                def s0():
                    cpa(zt[1][:, 0:2 * QN], pacc[1][:, 0:2 * QN], ["pacc1"], ["zt1"])
                    cpa(zt[0][:, 0:2 * QN], pacc[0][:, 0:2 * QN], ["pacc0"], ["zt0"])
                    P.op("dve", lambda e: e.reciprocal(out=zt[1][:, 0:2 * QN], in_=zt[1][:, 0:2 * QN]), ["zt1"], ["zt1"])
                    tt(zt[0][:, 0:2 * QN], zt[0][:, 0:2 * QN], zt[1][:, 0:2 * QN], ALU.mult, ["zt0", "zt1"], ["zt0"])
                    stt(of32[:, 0:QN], zt[0][:, QN:2 * QN], lamv[:, l, 0:1], zt[0][:, 0:QN], ALU.mult, ALU.add, ["zt0", "lamv"], ["of32"])
                    tt(sqb[:, 0:QN], of32[:, 0:QN], of32[:, 0:QN], ALU.mult, ["of32"], ["sqb"])

                def s1():
                    ps, pk = mm_bank()
                    box["ps"] = (ps, pk)
                    mm(ps[:, 0:QN], ones_b[:, 0:128], sqb[:, 0:QN], True, True, ["ones", "sqb"], [pk])
                    ts(zt[2][:, 0:QN], ps[:, 0:QN], 1.0 / 128, EPS, ALU.mult, ALU.add, [pk], ["zt2"])

                def s2():
                    rsqrt_small(zt[2][:, 0:QN], QN, "zt2")

                def s3():
                    stt(OT[0][:, h, 0:QN], of32[:, 0:QN], gdo[:, l, 0:1], zt[2][:, 0:QN], ALU.mult, ALU.mult, ["of32", "gdo", "zt2"], ["OT0"])
                return [s0, s1, s2, s3]

            for h in range(4):
                kget = lazy(lambda h=h: load_k(KdT_d[h, :, 0:nk * 128], 128, "KdT"))
                vget = lazy(lambda h=h: load_v(Vd_d[0:nk * 128, h * 128:(h + 1) * 128], "Vd"))
                for m in range(2):
                    add_head(kget, vget, QdT[:, h * 2 + m, 0:QN], "QdT", m * QN, sc_d, fin_diff(h) if m == 1 else None)
            kget_g = lazy(lambda: load_k(KgT_d[:, 0:nk * 128], 128, "KgT"))
            vget_g = lazy(lambda: load_v(Vg_d[0:nk * 128, :], "Vg"))
            for j in range(4):
                for kv in range(2):
                    add_head(kget_g, vget_g, QgT[:, j * 2 + kv, 0:QN], "QgT", kv * QN, sc_d, fin_pair(OT[1], "OT1", j) if kv == 1 else None)
            for jj in range(4):
                vget = lazy(lambda jj=jj: load_v(Vm_d[0:nk * 128, jj * 128:(jj + 1) * 128], "Vm"))
                for h in (2 * jj, 2 * jj + 1):
                    kget = lazy(lambda h=h: load_k(KmT_d[h, :, 0:nk * 128], 96, "KmT"))
                    add_head(kget, vget, QmT[:, h, 0:QN], "QmT", (h % 2) * QN, sc_m, fin_pair(OT[2], "OT2", jj) if h % 2 == 1 else None)
            DEPTH_P = 2
            FSTEP = 3 if nk >= 12 else 0
            pend = []
            deferred = []
            tick = [0]

            def run_due(all_=False):
                keep = []
                for (due, fn_) in deferred:
                    if all_ or due <= tick[0]:
                        fn_()
                    else:
                        keep.append((due, fn_))
                deferred[:] = keep

            def retire(item):
                pv0, fin0, st0 = item
                pv0(st0)
                if fin0 is not None:
                    for k_, fn_ in enumerate(fin0):
                        if k_ == 0:
                            fn_()
                        else:
                            deferred.append((tick[0] + k_ * FSTEP, fn_))

            for (score, pv, fin) in steps:
                st_ = score()
                pend.append((pv, fin, st_))
                if len(pend) > DEPTH_P:
                    retire(pend.pop(0))
                tick[0] += 1
                run_due()
            while pend:
                retire(pend.pop(0))
                tick[0] += 1
                run_due()
            run_due(all_=True)
            for b in range(3):
                for jj in range(2):
                    wg, wgk = wtile("wG", l, 0, 8, b * 1024 + jj * 512, 512)
                    wb_, wbk = wtile("wbr%d" % b, l, 0, 4, jj * 512, 512)
                    for j4 in range(4):
                        j = jj * 4 + j4
                        psg, pkg = mm_bank()
                        for kc in range(8):
                            mm(psg[:, 0:QN], wg[:, kc, j4 * 128:(j4 + 1) * 128], hT[:, kc, 0:QN], kc == 0, kc == 7, [wgk, "hT"], [pkg])
                        psb, pkb = mm_bank()
                        for kc in range(4):
                            mm(psb[:, 0:QN], wb_[:, kc, j4 * 128:(j4 + 1) * 128], OT[b][:, kc, 0:QN], kc == 0, kc == 3, [wbk, "OT%d" % b], [pkb])
                        si = nxt("sg", 2)
                        sg = sig[si]
                        sgk = "sig%d" % si
                        act(sg[:, 0:QN], psg[:, 0:QN], AF.Sigmoid, [pkg, "bgT"], [sgk], bias=bgT[:, l, b * 8 + j:b * 8 + j + 1])
                        yk = "ysum%d" % j
                        if b == 0:
                            tt(ysum[:, j, 0:QN], sg[:, 0:QN], psb[:, 0:QN], ALU.mult, [sgk, pkb], [yk])
                        elif b == 1:
                            tt(sg[:, 0:QN], sg[:, 0:QN], psb[:, 0:QN], ALU.mult, [sgk, pkb], [sgk])
                            tt(ysum[:, j, 0:QN], ysum[:, j, 0:QN], sg[:, 0:QN], ALU.add, [yk, sgk], [yk])
                        else:
                            tt(sg[:, 0:QN], sg[:, 0:QN], psb[:, 0:QN], ALU.mult, [sgk, pkb], [sgk])
                            tt(yT[:, j, 0:QN], ysum[:, j, 0:QN], sg[:, 0:QN], ALU.add, [yk, sgk], ["yT"])
            for ct in range(2):
                wt, wk = wtile("wout", l, 0, 8, ct * 512, 512)
                for i in range(nsub):
                    ps, pk = mm_bank()
                    for kc in range(8):
                        mm(ps[:], yT[:, kc, i * 128:(i + 1) * 128], wt[:, kc, :], kc == 0, kc == 7, ["yT", wk], [pk])
                    tt(zt[0][:], ps[:], g1b[:, ct * 512:(ct + 1) * 512], ALU.mult, [pk, "g1b"], ["zt0"])
                    xs = xbuf[:, sub0 + i, ct * 512:(ct + 1) * 512]
                    tt(xs, xs, zt[0][:], ALU.add, [xk[i], "zt0"], [xk[i]])

        def ffn():
            for i in range(nsub):
                rms_hT(xbuf[:, sub0 + i, :], xk[i], l, 1, mslot, i * 128, dst=h2T, dkeys=H2K)
            yield
            for t in range(6):
                ncol = 512 if t < 5 else 256
                wgt, wgk = wtile("wfi", l, 0, 8, t * 512, ncol)
                wut, wuk = wtile("wfi", l, 0, 8, FFN + t * 512, ncol)
                for j4 in range(ncol // 128):
                    j = t * 4 + j4
                    psg, pkg = mm_bank()
                    for kc in range(8):
                        mm(psg[:, 0:QN], wgt[:, kc, j4 * 128:(j4 + 1) * 128], h2T[:, kc, 0:QN], kc == 0, kc == 7, [wgk] + H2K, [pkg])
                    psu, pku = mm_bank()
                    for kc in range(8):
                        mm(psu[:, 0:QN], wut[:, kc, j4 * 128:(j4 + 1) * 128], h2T[:, kc, 0:QN], kc == 0, kc == 7, [wuk] + H2K, [pku])
                    si = nxt("sg", 2)
                    sg = sig[si]
                    sgk = "sig%d" % si
                    act(sg[:, 0:QN], psg[:, 0:QN], AF.Silu, [pkg], [sgk])
                    tt(hid[:, j, 0:QN], sg[:, 0:QN], psu[:, 0:QN], ALU.mult, [sgk, pku], HIDK)
                yield
            for ct in range(2):
                for (k0, nkc) in ((0, 8), (8, 8), (16, 6)):
                    wt, wk = wtile("wfo", l, k0, nkc, ct * 512, 512)
                    for i in range(nsub):
                        for kk in range(nkc):
                            kc = k0 + kk
                            mm(pacc[i][:], hid[:, kc, i * 128:(i + 1) * 128], wt[:, kk, :], kc == 0, kc == 21, HIDK + [wk], ["pacc%d" % i])
                    yield
                for i in range(nsub):
                    tt(zt[0][:], pacc[i][:], g2b[:, ct * 512:(ct + 1) * 512], ALU.mult, ["pacc%d" % i, "g2b"], ["zt0"])
                    xs = xbuf[:, sub0 + i, ct * 512:(ct + 1) * 512]
                    tt(xs, xs, zt[0][:], ALU.add, [xk[i], "zt0"], [xk[i]])

        return qphase, attn_post, ffn

    rot.update({"k2": 0, "v2": 0, "sg": 0})

    for s in range(nseq):
        for t in range(16):
            dma_sp(x_sb[:, t, :], x_d[s, t * 128:(t + 1) * 128, :], "xl", [], ["x%d" % t])
        for t in range(2):
            dma_sp(xc_sb[:, t, :], ctx_d[s, t * 128:(t + 1) * 128, :], "xl", [], ["xc%d" % t])
        _last = P.ins[-1]
        for t in range(16):
            P.R("x%d" % t).w = _last
        for t in range(2):
            P.R("xc%d" % t).w = _last
        for l in range(depth):
            first = (s == 0)
            dma_sp(ggq[:], ggq_d[l], "c13", (), ["gvec"])
            dma_sp(ggk[:], ggk_d[l], "c14", (), ["gvec"])
            dma_sp(gmq[:], gmq_d[l], "c15", (), ["gvec"])
            dma_sp(gmkv[:], gmkv_d[l], "c16", (), ["gvec"])
            if first and l == 0:
                drain_casts(0, 36)
            phase_a_tile(l, xc_sb, "xc", 0, 2, 0, False, 0, 2)
            for q in range(8):
                if first and l == 0:
                    drain_casts(0, 35)
                phase_a_tile(l, x_sb, "x", q * 2, 2, CTX + q * 256, True, q * 2, s)
            if first and l == 0:
                drain_casts(0, 1000)
            def run_all(g):
                for _ in g:
                    pass

            if l < depth - 1:
                bcast_gates(l, 2)
                qp, ap_, ff = phase_b_tile(l, xc_sb, "xc", 0, 2, False, 0, 2, 2)
                run_all(qp())
                ap_()
                run_all(ff())
            bcast_gates(l, s)
            NQ = SEQ // QT
            tiles = [phase_b_tile(l, x_sb, "x", q * NS, NS, True, q * NS, s, NKEY // 128) for q in range(NQ)]
            run_all(tiles[0][0]())
            for q in range(NQ):
                tiles[q][1]()
                gf = tiles[q][2]()
                gq = tiles[q + 1][0]() if q + 1 < NQ else iter(())
                next(gf, None)
                next(gf, None)
                alive = True
                while alive:
                    a1 = next(gq, "done")
                    a2 = next(gf, "done")
                    alive = not (a1 == "done" and a2 == "done")
                if first and l == 0 and depth > 1:
                    drain_casts(1, 40 if q < 7 else 1000)
        dma_sp(g2b[:], gfb_d, "c7", (), ["g2b"])
        for t in range(16):
            act(junk[:], x_sb[:, t, :], AF.Square, ["x%d" % t], ["junk", "st4"], accum_out=st[:, 4:5])
            ts(st[:, 5:6], st[:, 4:5], 1.0 / D, EPS, ALU.mult, ALU.add, ["st4"], ["st5"])
            rsqrt_small(st[:, 5:6], 1, "st5")
            oi = nxt("os", 2)
            ok_ = ["ysum%d" % (4 * oi + q_) for q_ in range(4)]
            stt(outst[oi], x_sb[:, t, :], st[:, 5:6], g2b[:], ALU.mult, ALU.mult, ["x%d" % t, "st5", "g2b"], ok_)
            dma_sp(out_d[s, t * 128:(t + 1) * 128, :], outst[oi], "o%d" % oi, ok_, [], is_out=True)
    P.finalize()
    return nc, P


_IN_SPLITS = np.cumsum([512, 512, 512, 512, 128, 128, 384, 256, 32, 3072])


def _tables():
    t = np.arange(SEQ)
    row = (t // 64).astype(np.float32)
    col = (t % 64).astype(np.float32)

    def tab(dim):
        a = dim // 2
        fr = (THETA ** (-np.arange(0, a, 2, dtype=np.float32) / a)).astype(np.float32)
        ar = row[:, None] * fr
        ac = col[:, None] * fr
        c = np.concatenate([np.cos(ar), np.cos(ac)], 1).astype(np.float32)
        s_ = np.concatenate([np.sin(ar), np.sin(ac)], 1).astype(np.float32)
        n = c.shape[1]
        return (np.ascontiguousarray(c.reshape(16, 128, n).transpose(1, 0, 2)),
                np.ascontiguousarray(s_.reshape(16, 128, n).transpose(1, 0, 2)))
    return tab(64), tab(32)


def _prep_shared(inp):
    f = np.float32
    w_in = inp["w_in"]
    sp = _IN_SPLITS
    dq, dk, dv, gq, gk, gv, mcq, mckv, mkr, gt = np.split(w_in, sp[:-1], axis=-1)
    sh = {}
    sh["w_mod"] = np.ascontiguousarray(inp["w_mod"], f)
    sh["b_mod3"] = np.ascontiguousarray(np.broadcast_to(inp["b_mod"][:, None, :], (2, 3, 6 * D)), f)
    tr = lambda v: np.ascontiguousarray(v.reshape(2, -1, 128).transpose(0, 2, 1), f)
    sh["g_norm1T"] = tr(inp["g_norm1"])
    sh["g_norm2T"] = tr(inp["g_norm2"])
    sh["wA"] = np.ascontiguousarray(np.concatenate([dk, dv, gk, gv, mckv, mkr], -1), f)
    sh["wQ"] = np.ascontiguousarray(np.concatenate([dq, gq, mcq], -1), f)
    sh["wG"] = np.ascontiguousarray(gt, f)
    sh["b_gateT"] = tr(inp["b_gate"])
    lam = np.stack([inp["lam_q1"], inp["lam_k1"], inp["lam_q2"], inp["lam_k2"]], 1)
    sh["lamb"] = np.ascontiguousarray(np.broadcast_to(lam[:, :, None, :], (2, 4, 128, 64)), f)
    sh["g_diff_outT"] = np.ascontiguousarray(inp["g_diff_out"].reshape(2, 128, 1), f)
    bc = lambda v: np.ascontiguousarray(np.broadcast_to(v[:, None, :], (2, 128, v.shape[-1])), f)
    sh["g_gqa_qb"] = bc(inp["g_gqa_q"])
    sh["g_gqa_kb"] = bc(inp["g_gqa_k"])
    sh["g_mla_qb"] = bc(inp["g_mla_q"])
    sh["g_mla_kvb"] = bc(inp["g_mla_kv"])
    sh["w_uq"] = np.ascontiguousarray(inp["w_mla_uq"], f)
    ukv = inp["w_mla_ukv"].reshape(2, 256, 8, 128)
    sh["w_ukv2"] = np.ascontiguousarray(np.concatenate([ukv[..., :64].reshape(2, 256, 512), ukv[..., 64:].reshape(2, 256, 512)], -1), f)
    sh["w_br0"] = np.ascontiguousarray(inp["w_br_diff"], f)
    sh["w_br1"] = np.ascontiguousarray(inp["w_br_gqa"], f)
    sh["w_br2"] = np.ascontiguousarray(inp["w_br_mla"], f)
    sh["w_out"] = np.ascontiguousarray(inp["w_out"], f)
    sh["w_ffn_in"] = np.ascontiguousarray(inp["w_ffn_in"], f)
    sh["w_ffn_out"] = np.ascontiguousarray(inp["w_ffn_out"], f)
    sh["g_finalb"] = np.ascontiguousarray(np.broadcast_to(inp["g_final"][None, :], (128, D)), f)
    sh["ident"] = np.eye(128, dtype=f)
    sel = np.zeros((3, 3, 128), f)
    for k in range(3):
        sel[k, k, :] = 1.0
    sh["sel3"] = sel
    (qc, qs), (mc, ms) = _tables()
    sh["tab_qk_cos"], sh["tab_qk_sin"], sh["tab_m_cos"], sh["tab_m_sin"] = qc, qs, mc, ms
    return sh


def _core_inputs(inp, sh, seqs):
    f = np.float32
    m = dict(sh)
    m["x"] = np.ascontiguousarray(inp["x"][seqs], f)
    m["ctx"] = np.ascontiguousarray(inp["ctx"][seqs], f)
    cs = [inp["c"][s] for s in seqs]
    while len(cs) < 2:
        cs.append(cs[0])
    c3 = np.stack(cs + [inp["c_ctx"]], 0)
    m["cT"] = np.ascontiguousarray(c3.reshape(3, 8, 128).transpose(2, 1, 0), f)
    return m


_CACHE = {}


def kernel(**inputs):
    inp = {k: np.asarray(v) for k, v in inputs.items()}
    ncores = 8
    nseq = 2
    if "prog" not in _CACHE:
        _CACHE["prog"] = build_program(nseq=nseq, depth=2)[0]
    nc = _CACHE["prog"]
    sh = _prep_shared(inp)
    in_maps = [_core_inputs(inp, sh, list(range(c * nseq, (c + 1) * nseq))) for c in range(ncores)]
    res = run_bass_kernel_spmd(nc, in_maps, core_ids=list(range(ncores)))
    out = np.concatenate([np.asarray(r["out"], np.float32) for r in res.results], axis=0)
    return out
```
